# Optimizing a Trainium2 kernel written in Bass

```python
import jax, jax.numpy as jnp
from jax import lax
import numpy as np

D_MODEL = 1024
BATCH = 8
SEQ = 4096
DEPTH = 2

MIX_WIDTH = D_MODEL
LRU_WIDTH = MIX_WIDTH // 2
LRU_HEADS = 8
LRU_HEAD_DIM = LRU_WIDTH // LRU_HEADS
LRU_CONV_WIDTH = 4
LRU_C = 8.0
SB_HEADS = 8
SB_HEAD_DIM = (MIX_WIDTH - LRU_WIDTH) // SB_HEADS
SB_WIDTH = SB_HEADS * SB_HEAD_DIM
Q_BLOCK = 128
IN_PROJ_WIDTH = 2 * LRU_WIDTH + 3 * SB_WIDTH
CONF_WIDTH = D_MODEL
CONF_CONV_WIDTH = 31
D_FF = 4 * D_MODEL
RMS_EPS = 1e-6
LN_EPS = 1e-5
N_EVEN = (DEPTH + 1) // 2
N_ODD = DEPTH // 2

kernel_name = "hybrid_rglru_stickbreak_conformer_trunk"


def rms_norm(x, g):
    xf = x.astype(jnp.float32)
    y = xf * lax.rsqrt(jnp.mean(xf * xf, axis=-1, keepdims=True) + RMS_EPS)
    return (y * g.astype(jnp.float32)).astype(x.dtype)


def layer_norm(x, g, b):
    xf = x.astype(jnp.float32)
    mu = jnp.mean(xf, axis=-1, keepdims=True)
    xc = xf - mu
    var = jnp.mean(xc * xc, axis=-1, keepdims=True)
    y = xc * lax.rsqrt(var + LN_EPS) * g.astype(jnp.float32) + b.astype(jnp.float32)
    return y.astype(x.dtype)


def causal_depthwise_conv(x, w, b):
    K, C = w.shape
    y = lax.conv_general_dilated(
        x, w[:, None, :].astype(x.dtype), window_strides=(1,), padding=[(K - 1, 0)],
        dimension_numbers=('NWC', 'WIO', 'NWC'), feature_group_count=C)
    return y + b


def rg_lru(x, gate_a_w, gate_a_b, gate_x_w, gate_x_b, lam):
    B, S, _ = x.shape
    xh = x.reshape(B, S, LRU_HEADS, LRU_HEAD_DIM)
    r = jax.nn.sigmoid(jnp.einsum('bshi,hij->bshj', xh, gate_a_w).reshape(B, S, LRU_WIDTH) + gate_a_b).astype(jnp.float32)
    i = jax.nn.sigmoid(jnp.einsum('bshi,hij->bshj', xh, gate_x_w).reshape(B, S, LRU_WIDTH) + gate_x_b).astype(jnp.float32)
    log_a = LRU_C * r * jax.nn.log_sigmoid(lam.astype(jnp.float32))
    a = jnp.exp(log_a)
    inp = jnp.sqrt(jnp.maximum(-jnp.expm1(2.0 * log_a), 0.0)) * (i * x.astype(jnp.float32))

    def combine(left, right):
        a_l, b_l = left
        a_r, b_r = right
        return a_l * a_r, a_r * b_l + b_r

    _, h = lax.associative_scan(combine, (a, inp), axis=1)
    return h.astype(x.dtype)


def stick_breaking_attention(q, k, v):
    B, S, H, Dh = q.shape
    scale = Dh ** -0.5
    out_blocks = []
    for blk in range(S // Q_BLOCK):
        q0 = blk * Q_BLOCK
        q1 = q0 + Q_BLOCK
        z = jnp.einsum('bthd,bshd->bhts', q[:, q0:q1].astype(jnp.float32),
                       k[:, :q1].astype(jnp.float32)) * scale
        t_pos = jnp.arange(q0, q1)[:, None]
        s_pos = jnp.arange(q1)[None, :]
        causal = s_pos < t_pos
        log_keep = jnp.where(causal, jax.nn.log_sigmoid(-z), 0.0)
        log_remain = lax.cumsum(log_keep, axis=3, reverse=True) - log_keep
        w = jnp.where(causal, jnp.exp(jax.nn.log_sigmoid(z) + log_remain), 0.0)
        out_blocks.append(jnp.einsum('bhts,bshd->bthd', w, v[:, :q1].astype(jnp.float32)))
    return jnp.concatenate(out_blocks, axis=1).astype(q.dtype)


def lru_sb_mixer(h, w_in, conv_w, conv_b, ga_w, ga_b, gx_w, gx_b, lam, w_out):
    B, S, _ = h.shape
    proj = h @ w_in
    gate, x_rec, q, k, v = jnp.split(
        proj, [LRU_WIDTH, 2 * LRU_WIDTH, 2 * LRU_WIDTH + SB_WIDTH, 2 * LRU_WIDTH + 2 * SB_WIDTH], axis=-1)
    x_rec = causal_depthwise_conv(x_rec, conv_w, conv_b)
    y_lru = jax.nn.gelu(gate) * rg_lru(x_rec, ga_w, ga_b, gx_w, gx_b, lam)
    y_sb = stick_breaking_attention(
        q.reshape(B, S, SB_HEADS, SB_HEAD_DIM),
        k.reshape(B, S, SB_HEADS, SB_HEAD_DIM),
        v.reshape(B, S, SB_HEADS, SB_HEAD_DIM)).reshape(B, S, SB_WIDTH)
    return jnp.concatenate([y_lru, y_sb], axis=-1) @ w_out


def conformer_conv_module(h, pw1_w, pw1_b, dw_w, dw_b, ln_g, ln_b, pw2_w, pw2_b):
    u = h @ pw1_w + pw1_b
    u = u[..., :CONF_WIDTH] * jax.nn.sigmoid(u[..., CONF_WIDTH:])
    u = causal_depthwise_conv(u, dw_w, dw_b)
    u = jax.nn.silu(layer_norm(u, ln_g, ln_b))
    return u @ pw2_w + pw2_b


def sq_relu_mlp(h, w1, w2):
    a = jax.nn.relu(h @ w1)
    return (a * a) @ w2


def setup_inputs(seed: int = 0) -> dict:
    key = jax.random.key(seed)
    ks = iter(jax.random.split(key, 40))
    f32 = jnp.float32

    def nrm(shape, fan_in):
        return jax.random.normal(next(ks), shape, f32) * (fan_in ** -0.5)

    def gain(shape):
        return 1.0 + 0.02 * jax.random.normal(next(ks), shape, f32)

    def bias(shape):
        return 0.02 * jax.random.normal(next(ks), shape, f32)

    x = jax.random.normal(next(ks), (BATCH, SEQ, D_MODEL), f32)
    ev_norm_g = gain((N_EVEN, D_MODEL))
    ev_w_in = nrm((N_EVEN, D_MODEL, IN_PROJ_WIDTH), D_MODEL)
    ev_conv_w = nrm((N_EVEN, LRU_CONV_WIDTH, LRU_WIDTH), LRU_CONV_WIDTH)
    ev_conv_b = bias((N_EVEN, LRU_WIDTH))
    ev_gate_a_w = nrm((N_EVEN, LRU_HEADS, LRU_HEAD_DIM, LRU_HEAD_DIM), LRU_HEAD_DIM)
    ev_gate_a_b = bias((N_EVEN, LRU_WIDTH))
    ev_gate_x_w = nrm((N_EVEN, LRU_HEADS, LRU_HEAD_DIM, LRU_HEAD_DIM), LRU_HEAD_DIM)
    ev_gate_x_b = bias((N_EVEN, LRU_WIDTH))
    u = jax.random.uniform(next(ks), (N_EVEN, LRU_WIDTH), f32, 0.9, 0.999)
    a0 = u ** (1.0 / LRU_C)
    ev_lam = jnp.log(a0) - jnp.log1p(-a0)
    ev_w_out = nrm((N_EVEN, MIX_WIDTH, D_MODEL), MIX_WIDTH)
    od_norm_g = gain((N_ODD, D_MODEL))
    od_pw1_w = nrm((N_ODD, D_MODEL, 2 * CONF_WIDTH), D_MODEL)
    od_pw1_b = bias((N_ODD, 2 * CONF_WIDTH))
    od_dw_w = nrm((N_ODD, CONF_CONV_WIDTH, CONF_WIDTH), CONF_CONV_WIDTH)
    od_dw_b = bias((N_ODD, CONF_WIDTH))
    od_ln_g = gain((N_ODD, CONF_WIDTH))
    od_ln_b = bias((N_ODD, CONF_WIDTH))
    od_pw2_w = nrm((N_ODD, CONF_WIDTH, D_MODEL), CONF_WIDTH)
    od_pw2_b = bias((N_ODD, D_MODEL))
    mlp_norm_g = gain((DEPTH, D_MODEL))
    mlp_w1 = nrm((DEPTH, D_MODEL, D_FF), D_MODEL)
    mlp_w2 = nrm((DEPTH, D_FF, D_MODEL), D_FF)
    final_g = gain((D_MODEL,))
    return {
        "x": x,
        "ev_norm_g": ev_norm_g, "ev_w_in": ev_w_in, "ev_conv_w": ev_conv_w, "ev_conv_b": ev_conv_b,
        "ev_gate_a_w": ev_gate_a_w, "ev_gate_a_b": ev_gate_a_b, "ev_gate_x_w": ev_gate_x_w,
        "ev_gate_x_b": ev_gate_x_b, "ev_lam": ev_lam, "ev_w_out": ev_w_out,
        "od_norm_g": od_norm_g, "od_pw1_w": od_pw1_w, "od_pw1_b": od_pw1_b, "od_dw_w": od_dw_w,
        "od_dw_b": od_dw_b, "od_ln_g": od_ln_g, "od_ln_b": od_ln_b, "od_pw2_w": od_pw2_w,
        "od_pw2_b": od_pw2_b,
        "mlp_norm_g": mlp_norm_g, "mlp_w1": mlp_w1, "mlp_w2": mlp_w2, "final_g": final_g,
    }


def reference(x, ev_norm_g, ev_w_in, ev_conv_w, ev_conv_b, ev_gate_a_w, ev_gate_a_b,
              ev_gate_x_w, ev_gate_x_b, ev_lam, ev_w_out,
              od_norm_g, od_pw1_w, od_pw1_b, od_dw_w, od_dw_b, od_ln_g, od_ln_b,
              od_pw2_w, od_pw2_b, mlp_norm_g, mlp_w1, mlp_w2, final_g):
    for layer in range(DEPTH):
        j = layer // 2
        if layer % 2 == 0:
            h = rms_norm(x, ev_norm_g[j])
            x = x + lru_sb_mixer(h, ev_w_in[j], ev_conv_w[j], ev_conv_b[j], ev_gate_a_w[j],
                                 ev_gate_a_b[j], ev_gate_x_w[j], ev_gate_x_b[j], ev_lam[j],
                                 ev_w_out[j])
        else:
            h = rms_norm(x, od_norm_g[j])
            x = x + conformer_conv_module(h, od_pw1_w[j], od_pw1_b[j], od_dw_w[j], od_dw_b[j],
                                          od_ln_g[j], od_ln_b[j], od_pw2_w[j], od_pw2_b[j])
        h = rms_norm(x, mlp_norm_g[layer])
        x = x + sq_relu_mlp(h, mlp_w1[layer], mlp_w2[layer])
    return rms_norm(x, final_g)
```

```python
from contextlib import ExitStack

import numpy as np
import concourse.bass as bass
import concourse.mybir as mybir
from concourse.bass_utils import run_bass_kernel_spmd

F32 = mybir.dt.float32
BF16 = mybir.dt.bfloat16
AF = mybir.ActivationFunctionType
ALU = mybir.AluOpType

D = 1024
S = 4096
T = 512
NSLOT = 3
NEG = -30000.0
ENGS = ("tensor", "vector", "scalar", "gpsimd", "sync")

R_EVG, R_ODG, R_MG0, R_MG1, R_FG, R_DWB, R_LNG, R_LNB, R_PW2B, R_PW1B0, R_PW1B1, R_DWW = 0, 1, 2, 3, 4, 5, 6, 7, 8, 9, 10, 11
NPA = 42


class Sem:
    def __init__(self, h):
        self.h = h
        self.count = 0


class Op:
    __slots__ = ("eng", "fn", "deps", "sem", "val", "dma", "used", "tile")

    def __init__(self, eng, fn, deps, dma_sem, tile):
        self.eng = eng
        self.fn = fn
        self.deps = deps
        self.sem = dma_sem
        self.val = None
        self.dma = dma_sem is not None
        self.used = False
        self.tile = tile


class Buf:
    __slots__ = ("name", "w", "r")

    def __init__(self, name):
        self.name = name
        self.w = None
        self.r = []


class Prog:
    def __init__(self):
        self.ops = {e: [] for e in ENGS}
        self.tile = 0

    def op(self, eng, fn, reads=(), writes=(), dma_sem=None, extra=(), nochain=False):
        deps = set(extra)
        for b in reads:
            if b.w is not None:
                deps.add(b.w)
        for b in writes:
            if b.w is not None:
                deps.add(b.w)
            deps.update(b.r)
        o = Op(eng, fn, deps, dma_sem, self.tile)
        deps.discard(o)
        if dma_sem is not None:
            deps = {d for d in deps if not (d.dma and d.sem is dma_sem and d.eng == eng)} if nochain else deps
            o.deps = deps
        for b in reads:
            b.r.append(o)
        for b in writes:
            b.w = o
            b.r = []
        self.ops[eng].append(o)
        return o

    def finalize(self, engsems):
        for e in ENGS:
            for o in self.ops[e]:
                for d in o.deps:
                    if d.eng == "tensor" and o.eng == "tensor":
                        continue
                    d.used = True
        for e in ENGS:
            for o in self.ops[e]:
                if o.dma:
                    o.sem.count += 16
                    o.val = o.sem.count
                elif o.used:
                    s = engsems[e][o.tile]
                    s.count += 1
                    o.sem = s
                    o.val = s.count

    def emit(self, eng_name, eng):
        waited = {}
        for o in self.ops[eng_name]:
            need = {}
            for d in o.deps:
                if d.eng == "tensor" and o.eng == "tensor":
                    continue
                if d.sem is None:
                    continue
                if need.get(d.sem, 0) < d.val:
                    need[d.sem] = d.val
            for s, v in need.items():
                if waited.get(s, 0) < v:
                    eng.wait_ge(s.h, v)
                    waited[s] = v
            ins = o.fn(eng)
            if o.sem is not None and (o.dma or o.used):
                ins.then_inc(o.sem.h, 16 if o.dma else 1)


def slab_table():
    t = []
    for nb in range(5):
        t.append(("w_in", 0, 0, nb))
    for nb in range(2):
        t.append(("w_out", 0, 0, nb))

    def mlp(l):
        for half in range(2):
            for nb in range(4):
                t.append(("w1", l, 0, half * 4 + nb))
            for g in range(2):
                for ks in range(2):
                    t.append(("w2", l, half * 2 + ks, g))

    mlp(0)
    for nb in range(4):
        t.append(("pw1", 0, 0, nb))
    for nb in range(2):
        t.append(("pw2", 0, 0, nb))
    mlp(1)
    return t


SLABS = slab_table()
NSLAB = len(SLABS)


def build(NT=8, dbg=None):
    nc = bass.Bass("TRN2", target_bir_lowering=False)
    P = Prog()
    ntok = NT * T

    def din(name, shape):
        return nc.dram_tensor(name, shape, F32, kind="ExternalInput").ap()

    x_d = din("x", [S, D])
    w_d = {
        "w_in": din("w_in", [D, 2560]),
        "w_out": din("w_out", [D, D]),
        "w1": din("w1", [2, D, 4 * D]),
        "w2": din("w2", [2, 4 * D, D]),
        "pw1": din("pw1", [D, 2 * D]),
        "pw2": din("pw2", [D, D]),
    }
    pa_d = din("pa", [NPA, D])
    pb_d = din("pb", [8, 512])
    gaw_d = din("gaw", [8, 64, 64])
    gxw_d = din("gxw", [8, 64, 64])
    cst_d = din("cst", [128, 896])
    y_d = nc.dram_tensor("y", [S, D], F32, kind="ExternalOutput").ap()
    wscr = nc.dram_tensor("wscr", [NSLAB, 128, 4096], BF16, kind="Internal").ap()
    dbg_out = {}

    es = ExitStack()
    with es:
        def sb(name, shape, dt=F32):
            return es.enter_context(nc.sbuf_tensor(name, shape, dt))

        def newsem(name):
            return Sem(es.enter_context(nc.semaphore(name)))

        KT = sb("KT", [128, 4, S], BF16)
        Vt = sb("Vt", [128, 32, 512], BF16)
        xT = sb("xT", [128, 8, T])
        io_b = [sb(f"io{i}", [128, D]) for i in range(2)]
        xin = io_b
        ost = io_b
        hT = sb("hT", [128, 8, T], BF16)
        RA = sb("RA", [128, 8, T])
        XRb = sb("XRb", [128, 4, T + 3], BF16)
        QT = sb("QT", [128, 4, T], BF16)
        YC = [sb(f"YC{i}", [128, 8, T], BF16) for i in range(2)]
        YU = sb("YU", [128, 8, T + 30], BF16)
        ce_b = [sb(f"ce{i}", [128, T]) for i in range(2)]
        cw_b = [sb(f"cw{i}", [128, T], BF16) for i in range(2)]
        UH = sb("UH", [128, 8, 30], BF16)
        e2 = sb("e2", [128, 2, T])
        e_b = [e2[:, i, :] for i in range(2)]
        sp2 = [sb(f"sp2_{i}", [128, 2, T], BF16) for i in range(2)]
        w2 = [sb(f"w2_{i}", [128, 2, T], BF16) for i in range(2)]
        Srun = [sb(f"Srun{i}", [128, T], BF16) for i in range(2)]
        sq_b = [sb(f"sq{i}", [128, T], BF16) for i in range(2)]
        rstd = sb("rstd", [128, T])
        lnv = sb("lnv", [128, T])
        xcb = sq_b[0]
        ring = [sb(f"ring{i}", [128, 8, 512], BF16) for i in range(NSLOT)]
        cstf = sb("cstf", [128, 128])
        cstb = sb("cstb", [128, 896], BF16)
        onesb = sb("onesb", [128, 128], BF16)
        nonesb = sb("nonesb", [128, 128], BF16)
        PAT = sb("PAT", [128, 8, NPA])
        PBT = sb("PBT", [128, 4, 8])
        DER = sb("DER", [128, 48])
        BDb = sb("BDb", [128, 4, 2, 128], BF16)
        DG4 = sb("DG4", [128, 4, 4, 128], BF16)
        NDG = 4
        dg31 = sb("dg31", [128, NDG, 128], BF16)
        hcar = sb("hcar", [128, 4])
        tst = sb("tst", [128, 8])

        pas = RA[0:NPA, 0:2, :].rearrange("p a b -> p (a b)")
        pbs = RA[0:8, 2, :]
        BDf = RA[:, 3:5, :].rearrange("p a b -> p (a b)").rearrange("p (a b c) -> p a b c", a=4, b=2)
        xstA = [YU[:].rearrange("p a b -> p (a b)")[:, 2 * D * i:2 * D * (i + 1)].bitcast(F32) for i in range(2)]
        b_xstA = None
        IDF = cstf[:, 0:128]
        IDB = cstb[:, 0:128]
        NTRI = cstb[:, 128:256]
        NEGTRI = cstb[:, 256:384]
        NEGF3 = cstb[:, 384:896]


        def aT_chunk(k):
            v = RA[:, k // 2, :].bitcast(BF16)
            return v[:, (k % 2) * T:(k % 2 + 1) * T]

        NPB = 4
        NPG = 3
        NPO = 1
        PBt = es.enter_context(nc.psum_tensor("pbt", [128, NPB, T], F32))
        PB = [PBt[:, i, :] for i in range(NPB)]
        PO = [es.enter_context(nc.psum_tensor(f"po{i}", [128, T], F32)) for i in range(NPO)]
        PG = [es.enter_context(nc.psum_tensor(f"pg{i}", [128, T], F32)) for i in range(NPG)]

        engsems = {e: [newsem(f"s_{e}_{t}") for t in range(NT + 1)] for e in ("tensor", "vector", "scalar", "gpsimd")}
        engsems["sync"] = [None] * (NT + 1)
        ring_sem = [newsem(f"ring{i}") for i in range(NSLOT)]
        cast_sem = [newsem(f"cast{i}") for i in range(4)]
        xin_sem = [newsem(f"xin{i}") for i in range(2)]
        xa_sem = [newsem(f"xa{i}") for i in range(2)]
        ost_sem = [newsem(f"ost{i}") for i in range(2)]
        setup_sem = newsem("setup")
        setup_sem2 = newsem("setup2")

        def B(name):
            return Buf(name)

        b_KT = [[B(f"KT{j}_{t}") for t in range(NT)] for j in range(4)]
        b_V = [B(f"V{t}") for t in range(NT)]
        b_xT = [B(f"xT{c}") for c in range(8)]
        b_xin = [B("io0"), B("io1")]
        b_ost = b_xin
        b_hT = [B(f"hT{c}") for c in range(8)]
        b_RA = [B(f"RA{c}") for c in range(8)]
        b_XR = [B(f"XR{m}") for m in range(4)]
        b_XRh = B("XRh")
        b_QT = [B(f"QT{m}") for m in range(4)]
        b_YU = [B(f"YU{c}") for c in range(8)]
        b_YC = [[B(f"YC{i}_{c}") for c in range(8)] for i in range(2)]
        b_ce = [B("ce0"), B("ce1")]
        b_cw = [B("cw0"), B("cw1")]
        b_UH = B("UH")
        b_e = [B("e2")]
        b_sp = [B(f"sp2_{i}") for i in range(2)]
        b_w = [B(f"w2_{i}") for i in range(2)]
        b_S = [B("S0"), B("S1")]
        b_sq = [B("sq0"), B("sq1")]
        b_rstd = B("rstd")
        b_lnv = B("lnv")
        b_xcb = b_sq[0]
        b_ring = [B(f"ring{i}") for i in range(NSLOT)]
        b_PB = [B(f"PB{i}") for i in range(NPB)]
        b_PO = [B(f"PO{i}") for i in range(NPO)]
        b_PG = [B(f"PG{i}") for i in range(NPG)]
        b_wscr = [B(f"wscr{i}") for i in range(NSLAB)]
        b_cst = B("cst")
        b_par = B("params")
        b_dg = [B(f"dg{i}") for i in range(NDG)]
        b_hcar = B("hcar")
        b_tst = [B("tst0"), B("tst1")]
        b_stg = b_RA[0:5]

        def dma(q, out, in_, sem, reads=(), writes=(), nochain=False):
            return P.op(q, lambda e, out=out, in_=in_: e.dma_start(out=out, in_=in_), reads=reads, writes=writes, dma_sem=sem, nochain=nochain)

        dma("gpsimd", cstb[:], cst_d[:, :], setup_sem2, writes=[b_cst])
        cast_ops = []
        for i, (key, l, ks, nb) in enumerate(SLABS):
            W = w_d[key]
            W2 = W[l] if key in ("w1", "w2") else W
            src = W2[ks * 1024:(ks + 1) * 1024, nb * 512:(nb + 1) * 512].rearrange("(kc p) c -> p kc c", p=128)
            dst = wscr[i].rearrange("p (kc c) -> p kc c", c=512)
            cast_ops.append(P.op("gpsimd", lambda e, dst=dst, src=src: e.dma_start(out=dst, in_=src), writes=[b_wscr[i]], dma_sem=cast_sem[i % 4],
                                 extra=(cast_ops[i - 3:i - 2] if i >= 3 else ())))

        dma("sync", cstf[:], cst_d[:, 0:128], setup_sem, writes=[b_cst], nochain=True)
        dma("sync", pas[:], pa_d[:, :], setup_sem, writes=[b_par] + b_stg, nochain=True)
        dma("sync", pbs[:], pb_d[:, :], setup_sem, writes=[b_par] + b_stg, nochain=True)
        P.op("vector", lambda e: e.memset(BDf.rearrange("p a b c -> p (a b c)"), 0.0), writes=[b_par] + b_stg)
        for h in range(8):
            r0 = (h % 2) * 64
            dma("sync", BDf[r0:r0 + 64, h // 2, 0, r0:r0 + 64], gaw_d[h], setup_sem, writes=[b_par] + b_stg, nochain=True)
            dma("sync", BDf[r0:r0 + 64, h // 2, 1, r0:r0 + 64], gxw_d[h], setup_sem, writes=[b_par] + b_stg, nochain=True)

        P.op("vector", lambda e: e.memset(onesb[:], 1.0), writes=[b_cst])
        P.op("vector", lambda e: e.memset(nonesb[:], -1.0), writes=[b_cst])
        P.op("vector", lambda e: e.memset(hcar[:], 0.0), writes=[b_hcar])
        P.op("vector", lambda e: e.memset(XRb[:, :, 0:3], 0.0), writes=[b_XRh])
        P.op("vector", lambda e: e.memset(UH[:].rearrange("p a b -> p (a b)"), 0.0), writes=[b_UH])
        P.op("vector", lambda e: e.tensor_copy(out=BDb[:].rearrange("p a b c -> p (a b c)"), in_=BDf.rearrange("p a b c -> p (a b c)")), reads=[b_par] + b_stg, writes=[b_par])

        for c in range(8):
            P.op("tensor", lambda e, c=c: e.transpose(out=PG[0][:, c * NPA:(c + 1) * NPA], in_=pas[:, c * 128:(c + 1) * 128], identity=IDF[0:NPA, 0:NPA]),
                 reads=[b_par, b_cst] + b_stg, writes=[b_PG[0]])
        P.op("vector", lambda e: e.tensor_copy(out=PAT[:].rearrange("p c r -> p (c r)"), in_=PG[0][:, 0:8 * NPA]), reads=[b_PG[0]], writes=[b_par])
        for c in range(4):
            P.op("tensor", lambda e, c=c: e.transpose(out=PG[1][:, c * 8:(c + 1) * 8], in_=pbs[:, c * 128:(c + 1) * 128], identity=IDF[0:8, 0:8]),
                 reads=[b_par, b_cst] + b_stg, writes=[b_PG[1]])
        P.op("vector", lambda e: e.tensor_copy(out=PBT[:].rearrange("p c r -> p (c r)"), in_=PG[1][:, 0:32]), reads=[b_PG[1]], writes=[b_par])

        P.op("vector", lambda e: e.tensor_scalar(out=DER[:, 0:4], in0=PBT[:, :, 5], scalar1=-1.0, scalar2=None, op0=ALU.mult), reads=[b_par], writes=[b_par])
        P.op("vector", lambda e: e.tensor_scalar(out=DER[:, 4:8], in0=PBT[:, :, 6], scalar1=-1.0, scalar2=None, op0=ALU.mult), reads=[b_par], writes=[b_par])
        P.op("scalar", lambda e: e.activation(out=DER[:, 40:44], in_=PBT[:, :, 7], func=AF.Exp, scale=-1.0), reads=[b_par], writes=[b_par])
        P.op("scalar", lambda e: e.activation(out=DER[:, 44:48], in_=DER[:, 40:44], func=AF.Ln, bias=1.0), reads=[b_par], writes=[b_par])
        P.op("vector", lambda e: e.tensor_scalar(out=DER[:, 8:12], in0=DER[:, 44:48], scalar1=-8.0, scalar2=None, op0=ALU.mult), reads=[b_par], writes=[b_par])
        P.op("vector", lambda e: e.tensor_scalar(out=DER[:, 12:16], in0=DER[:, 44:48], scalar1=-16.0, scalar2=None, op0=ALU.mult), reads=[b_par], writes=[b_par])
        P.op("vector", lambda e: e.tensor_scalar(out=DER[:, 16:24], in0=PAT[:, :, R_PW1B1], scalar1=-1.0, scalar2=None, op0=ALU.mult), reads=[b_par], writes=[b_par])
        P.op("vector", lambda e: e.tensor_scalar(out=DER[:, 24:32], in0=PAT[:, :, R_LNG], scalar1=-1.0, scalar2=None, op0=ALU.mult), reads=[b_par], writes=[b_par])
        P.op("vector", lambda e: e.tensor_scalar(out=DER[:, 32:40], in0=PAT[:, :, R_LNB], scalar1=-1.0, scalar2=None, op0=ALU.mult), reads=[b_par], writes=[b_par])
        for m in range(4):
            for k in range(4):
                P.op("vector", lambda e, m=m, k=k: e.tensor_scalar(out=DG4[:, m, k, :], in0=IDB, scalar1=PBT[:, m, k:k + 1], scalar2=None, op0=ALU.mult),
                     reads=[b_par, b_cst], writes=[b_par])

        st = {"next_load": 0, "next_use": 0}
        seq = list(range(5))
        for t_ in range(NT):
            if t_ + 1 < NT:
                seq += list(range(5))
            seq += list(range(5, NSLAB))
        total_slabs = len(seq)

        def issue_load(gi):
            slot = gi % NSLOT
            i = seq[gi]
            dma("sync", ring[slot][:].rearrange("p a b -> p (a b)"), wscr[i], ring_sem[slot], reads=[b_wscr[i]], writes=[b_ring[slot]])

        for gi in range(min(NSLOT, total_slabs)):
            issue_load(gi)
        st["next_load"] = min(NSLOT, total_slabs)

        def next_slab(expect):
            gi = st["next_use"]
            assert SLABS[seq[gi]][0] == expect, (SLABS[seq[gi]], expect)
            st["next_use"] += 1
            return gi % NSLOT

        def release_slab():
            if st["next_load"] < total_slabs:
                issue_load(st["next_load"])
                st["next_load"] += 1

        rr = {"pg": 0, "ev": 0}

        def pgbank():
            rr["pg"] = (rr["pg"] + 1) % NPG
            return rr["pg"]

        def evac_copy(out_ap, in_ap, reads, writes, eng=None, scale=None):
            if eng is None:
                rr["ev"] ^= 1
                eng = "vector" if rr["ev"] else "scalar"
            if eng == "vector":
                if scale is None:
                    return P.op("vector", lambda e: e.tensor_copy(out=out_ap, in_=in_ap), reads=reads, writes=writes)
                return P.op("vector", lambda e: e.tensor_scalar(out=out_ap, in0=in_ap, scalar1=scale, scalar2=None, op0=ALU.mult), reads=reads, writes=writes)
            sc = 1.0 if scale is None else scale
            return P.op("scalar", lambda e: e.activation(out=out_ap, in_=in_ap, func=AF.Identity, scale=sc), reads=reads, writes=writes)

        def run(gen):
            for _ in gen:
                pass

        def load_x(tt, in_A=False):
            t0 = tt * T
            for s in range(4):
                bi = s % 2
                if in_A:
                    stg, bst, ssem = xstA[bi], b_hT[4 * bi:4 * bi + 4], xa_sem[bi]
                else:
                    stg, bst, ssem = xin[bi][:], [b_xin[bi]], xin_sem[bi]
                dma("sync", stg, x_d[t0 + s * 128:t0 + (s + 1) * 128, :], ssem, writes=bst)
                for half in range(2):
                    bank = pgbank()
                    for cc in range(4):
                        c = half * 4 + cc
                        P.op("tensor", lambda e, stg=stg, c=c, cc=cc, bank=bank: e.transpose(out=PG[bank][:, cc * 128:(cc + 1) * 128], in_=stg[:, c * 128:(c + 1) * 128], identity=IDF),
                             reads=list(bst) + [b_cst], writes=[b_PG[bank]])
                    P.op("vector", lambda e, half=half, s=s, bank=bank: e.tensor_copy(out=xT[:, half * 4:half * 4 + 4, s * 128:(s + 1) * 128], in_=PG[bank][:].rearrange("p (c t) -> p c t", t=128)),
                         reads=[b_PG[bank]], writes=[b_xT[half * 4 + i] for i in range(4)])
                yield

        def rms_norm():
            bank = pgbank()
            for c in range(8):
                q = c % 2
                P.op("scalar", lambda e, c=c, q=q: e.activation(out=sq_b[q][:], in_=xT[:, c, :], func=AF.Square), reads=[b_xT[c]], writes=[b_sq[q]])
                P.op("tensor", lambda e, c=c, q=q, bank=bank: e.matmul(PG[bank][:], onesb[:], sq_b[q][:], start=(c == 0), stop=(c == 7)),
                     reads=[b_sq[q], b_cst], writes=[b_PG[bank]])
            P.op("scalar", lambda e, bank=bank: e.activation(out=lnv[:], in_=PG[bank][:], func=AF.Ln, scale=1.0 / D, bias=1e-6), reads=[b_PG[bank]], writes=[b_lnv])
            P.op("scalar", lambda e: e.activation(out=rstd[:], in_=lnv[:], func=AF.Exp, scale=-0.5), reads=[b_lnv], writes=[b_rstd])

        def norm_apply_h(grow):
            for c in range(8):
                P.op("vector", lambda e, c=c: e.scalar_tensor_tensor(out=hT[:, c, :], in0=xT[:, c, :], scalar=PAT[:, c, grow:grow + 1], in1=rstd[:], op0=ALU.mult, op1=ALU.mult),
                     reads=[b_xT[c], b_rstd, b_par], writes=[b_hT[c]])

        def proj_fm(key, nslabs, src_chunks, b_src, evac):
            for sl in range(nslabs):
                slot = next_slab(key)
                for m in range(4):
                    bank = pgbank()
                    for kc in range(8):
                        P.op("tensor", lambda e, slot=slot, m=m, kc=kc, bank=bank: e.matmul(PG[bank][:], ring[slot][:, kc, m * 128:(m + 1) * 128], src_chunks(kc), start=(kc == 0), stop=(kc == 7)),
                             reads=[b_ring[slot], b_src[kc]], writes=[b_PG[bank]])
                    evac(sl, m, bank)
                    yield
                release_slab()

        pro_bank = {"forced": None}

        def gen_A_pro(tt):
            t0 = tt * T
            for s in range(4):
                bi = s % 2
                stg, bst, ssem = xstA[bi], (b_YU[0:4] if bi == 0 else b_YU[3:8]), xa_sem[bi]
                ssq, rs = tst[:, 4 * bi:4 * bi + 1], tst[:, 4 * bi + 1:4 * bi + 2]
                dma("sync", stg, x_d[t0 + s * 128:t0 + (s + 1) * 128, :], ssem, writes=bst)
                P.op("vector", lambda e, stg=stg, s=s: e.tensor_tensor(out=hT[:, :, s * 128:(s + 1) * 128], in0=stg.rearrange("p (c f) -> p c f", f=128), in1=stg.rearrange("p (c f) -> p c f", f=128), op=ALU.mult),
                     reads=list(bst), writes=b_hT)
                P.op("vector", lambda e, s=s, ssq=ssq: e.reduce_sum(out=ssq, in_=hT[:, :, s * 128:(s + 1) * 128], axis=mybir.AxisListType.XY), reads=b_hT, writes=[b_tst[bi]])
                P.op("scalar", lambda e, ssq=ssq, rs=rs: e.activation(out=rs, in_=ssq, func=AF.Ln, scale=1.0 / D, bias=1e-6), reads=[b_tst[bi]], writes=[b_tst[bi]])
                P.op("scalar", lambda e, rs=rs: e.activation(out=rs, in_=rs, func=AF.Exp, scale=-0.5), reads=[b_tst[bi]], writes=[b_tst[bi]])
                P.op("scalar", lambda e, stg=stg, rs=rs: e.activation(out=stg, in_=stg, func=AF.Identity, scale=rs), reads=list(bst) + [b_tst[bi]], writes=bst)
                yield
                for half in range(2):
                    bank = pro_bank["forced"] if pro_bank["forced"] is not None else pgbank()
                    for cc in range(4):
                        c = half * 4 + cc
                        P.op("tensor", lambda e, stg=stg, c=c, cc=cc, bank=bank: e.transpose(out=PG[bank][:, cc * 128:(cc + 1) * 128], in_=stg[:, c * 128:(c + 1) * 128], identity=IDF),
                             reads=list(bst) + [b_cst], writes=[b_PG[bank]])
                    for cc in range(4):
                        c = half * 4 + cc
                        if half == 0:
                            P.op("vector", lambda e, c=c, cc=cc, s=s, bank=bank: e.tensor_scalar(out=hT[:, c, s * 128:(s + 1) * 128], in0=PG[bank][:, cc * 128:(cc + 1) * 128], scalar1=PAT[:, c, R_EVG:R_EVG + 1], scalar2=None, op0=ALU.mult),
                                 reads=[b_PG[bank], b_par], writes=[b_hT[c]])
                        else:
                            P.op("scalar", lambda e, c=c, cc=cc, s=s, bank=bank: e.activation(out=hT[:, c, s * 128:(s + 1) * 128], in_=PG[bank][:, cc * 128:(cc + 1) * 128], func=AF.Identity, scale=PAT[:, c, R_EVG:R_EVG + 1]),
                                 reads=[b_PG[bank], b_par], writes=[b_hT[c]])
                    yield

        def stage_A(tt, pro_done=False):
            P.tile = tt
            t0 = tt * T
            yc = YC[tt % 2]
            b_yc = b_YC[tt % 2]
            if not pro_done:
                run(gen_A_pro(tt))
            if tt > 0:
                P.op("vector", lambda e: e.tensor_copy(out=XRb[:, :, 0:3], in_=XRb[:, :, T:T + 3]), reads=b_XR, writes=[b_XRh])

            def ev_in(sl, m, bank):
                if sl == 0:
                    G = RA[:, m, :]
                    bG = b_RA[m]
                    Tg = RA[:, 4 + m, :]
                    bTg = b_RA[4 + m]
                    evac_copy(G, PG[bank][:], [b_PG[bank]], [bG], eng="vector")
                    P.op("scalar", lambda e: e.activation(out=Tg, in_=G, func=AF.Square), reads=[bG], writes=[bTg])
                    P.op("vector", lambda e: e.tensor_scalar(out=Tg, in0=Tg, scalar1=0.044715, scalar2=1.0, op0=ALU.mult, op1=ALU.add), reads=[bTg], writes=[bTg])
                    P.op("vector", lambda e: e.tensor_tensor(out=Tg, in0=Tg, in1=G, op=ALU.mult), reads=[bTg, bG], writes=[bTg])
                    P.op("scalar", lambda e: e.activation(out=Tg, in_=Tg, func=AF.Exp, scale=-1.5957691216057308), reads=[bTg], writes=[bTg])
                    P.op("scalar", lambda e: e.activation(out=Tg, in_=Tg, func=AF.Ln, bias=1.0), reads=[bTg], writes=[bTg])
                    P.op("scalar", lambda e: e.activation(out=Tg, in_=Tg, func=AF.Exp, scale=-1.0), reads=[bTg], writes=[bTg])
                    P.op("vector", lambda e: e.tensor_tensor(out=G, in0=Tg, in1=G, op=ALU.mult), reads=[bTg, bG], writes=[bG])
                elif sl == 1:
                    evac_copy(XRb[:, m, 3:T + 3], PG[bank][:], [b_PG[bank], b_XRh], [b_XR[m]])
                elif sl == 2:
                    evac_copy(QT[:, m, :], PG[bank][:], [b_PG[bank]], [b_QT[m]], eng="vector", scale=0.125)
                else:
                    evac_copy(KT[:, m, t0:t0 + T], PG[bank][:], [b_PG[bank]], [b_KT[m][tt]], eng="vector")

            src_h = lambda kc: hT[:, kc, :]
            run(proj_fm("w_in", 2, src_h, b_hT, ev_in))

            def rest_gen():
                yield from proj_fm("w_in", 2, src_h, b_hT, lambda sl, m, bank: ev_in(sl + 2, m, bank))
                slot = next_slab("w_in")
                for s in range(4):
                    bank = pgbank()
                    for kc in range(8):
                        P.op("tensor", lambda e, slot=slot, s=s, kc=kc, bank=bank: e.matmul(PG[bank][:], hT[:, kc, s * 128:(s + 1) * 128], ring[slot][:, kc, :], start=(kc == 0), stop=(kc == 7)),
                             reads=[b_ring[slot], b_hT[kc]], writes=[b_PG[bank]])
                    evac_copy(Vt[:, tt * 4 + s, :], PG[bank][:], [b_PG[bank]], [b_V[tt]], eng="vector")
                    yield
                release_slab()

            def lru_gen():
                T0, T1, T2, T3 = (RA[:, 4, :], RA[:, 5, :], RA[:, 6, :], RA[:, 7, :])
                bT = b_RA[4:8]
                for m in range(4):
                    G = RA[:, m, :]
                    bG = b_RA[m]
                    bank = pgbank()
                    for k in range(4):
                        P.op("tensor", lambda e, m=m, k=k, bank=bank: e.matmul(PG[bank][:], DG4[:, m, k, :], XRb[:, m, k:k + T], start=(k == 0), stop=(k == 3)),
                             reads=[b_par, b_XR[m], b_XRh], writes=[b_PG[bank]])
                    P.op("scalar", lambda e, m=m, bank=bank: e.activation(out=T0, in_=PG[bank][:], func=AF.Identity, bias=PBT[:, m, 4:5]), reads=[b_PG[bank], b_par], writes=[bT[0]])
                    P.op("vector", lambda e: e.tensor_copy(out=xcb[:], in_=T0), reads=[bT[0]], writes=[b_xcb])
                    yield
                    P.op("tensor", lambda e, m=m: e.matmul(PB[0][:], BDb[:, m, 0, :], xcb[:], start=True, stop=True), reads=[b_par, b_xcb], writes=[b_PB[0]])
                    P.op("tensor", lambda e, m=m: e.matmul(PB[1][:], BDb[:, m, 1, :], xcb[:], start=True, stop=True), reads=[b_par, b_xcb], writes=[b_PB[1]])
                    yield
                    P.op("scalar", lambda e, m=m: e.activation(out=T1, in_=PB[0][:], func=AF.Exp, scale=-1.0, bias=DER[:, m:m + 1]), reads=[b_PB[0], b_par], writes=[bT[1]])
                    P.op("scalar", lambda e: e.activation(out=T1, in_=T1, func=AF.Ln, bias=1.0), reads=[bT[1]], writes=[bT[1]])
                    P.op("scalar", lambda e: e.activation(out=T1, in_=T1, func=AF.Exp, scale=-1.0), reads=[bT[1]], writes=[bT[1]])
                    P.op("scalar", lambda e, m=m: e.activation(out=T2, in_=T1, func=AF.Exp, scale=DER[:, 8 + m:9 + m]), reads=[bT[1], b_par], writes=[bT[2]])
                    P.op("scalar", lambda e, m=m: e.activation(out=T3, in_=T1, func=AF.Exp, scale=DER[:, 12 + m:13 + m]), reads=[bT[1], b_par], writes=[bT[3]])
                    P.op("scalar", lambda e: e.activation(out=T3, in_=T3, func=AF.Ln, scale=-1.0, bias=1.0), reads=[bT[3]], writes=[bT[3]])
                    P.op("scalar", lambda e: e.activation(out=T3, in_=T3, func=AF.Exp, scale=0.5), reads=[bT[3]], writes=[bT[3]])
                    P.op("scalar", lambda e, m=m: e.activation(out=T1, in_=PB[1][:], func=AF.Exp, scale=-1.0, bias=DER[:, 4 + m:5 + m]), reads=[b_PB[1], b_par], writes=[bT[1]])
                    P.op("scalar", lambda e: e.activation(out=T1, in_=T1, func=AF.Ln, bias=1.0), reads=[bT[1]], writes=[bT[1]])
                    P.op("scalar", lambda e: e.activation(out=T1, in_=T1, func=AF.Exp, scale=-1.0), reads=[bT[1]], writes=[bT[1]])
                    P.op("vector", lambda e: e.tensor_tensor(out=T1, in0=T1, in1=T0, op=ALU.mult), reads=[bT[1], bT[0]], writes=[bT[1]])
                    P.op("vector", lambda e: e.tensor_tensor(out=T3, in0=T3, in1=T1, op=ALU.mult), reads=[bT[3], bT[1]], writes=[bT[3]])
                    P.op("vector", lambda e, m=m: e.tensor_tensor_scan(out=T0, data0=T2, data1=T3, initial=hcar[:, m:m + 1], op0=ALU.mult, op1=ALU.add),
                         reads=[bT[2], bT[3], b_hcar], writes=[bT[0]])
                    P.op("vector", lambda e, m=m: e.tensor_copy(out=hcar[:, m:m + 1], in_=T0[:, T - 1:T]), reads=[bT[0]], writes=[b_hcar])
                    P.op("vector", lambda e, m=m, yc=yc, G=G: e.tensor_tensor(out=yc[:, m, :], in0=G, in1=T0, op=ALU.mult), reads=[bG, bT[0]], writes=[b_yc[m]])
                    yield


            g_r, g_l = rest_gen(), lru_gen()
            live = [g_r, g_l]
            while live:
                for g in list(live):
                    try:
                        next(g)
                    except StopIteration:
                        live.remove(g)

        def gen_B(tt):
            yc = YC[tt % 2]
            b_yc = b_YC[tt % 2]
            pairs = []
            for j in range(4):
                blocks = [(4 * tt + i, (0 if i == 3 else 128 * i), i) for i in (3, 2, 1, 0)]
                blocks += [(kb, 0, -1) for kb in range(4 * tt - 1, -1, -1)]
                for n, (kb, c0, di) in enumerate(blocks):
                    pairs.append([dict(h=2 * j + q, q=q, kb=kb, c0=c0, di=di, first=(n == 0), last=(n == len(blocks) - 1), i=2 * len(pairs) + q) for q in range(2)])
            npairs = len(pairs)

            def qk(s_):
                h, kb, c0, i = s_["h"], s_["kb"], s_["c0"], s_["i"]
                j, r0 = h // 2, (h % 2) * 64
                bk = i % NPB
                P.op("tensor", lambda e: e.matmul(PB[bk][:, c0:T], KT[r0:r0 + 64, j, kb * 128:(kb + 1) * 128], QT[r0:r0 + 64, j, c0:T], start=True, stop=False, skip_group_check=True),
                     reads=[b_KT[j][kb // 4], b_QT[j]], writes=[b_PB[bk]])

            def mask(s_):
                c0, i = s_["c0"], s_["i"]
                bk = i % NPB
                if s_["di"] == 3:
                    P.op("tensor", lambda e: e.matmul(PB[bk][:, 0:T], IDB, NEGF3, start=False, stop=False, skip_group_check=True), reads=[b_cst], writes=[b_PB[bk]])
                elif s_["di"] >= 0:
                    P.op("tensor", lambda e: e.matmul(PB[bk][:, c0:c0 + 128], IDB, NEGTRI, start=False, stop=False, skip_group_check=True), reads=[b_cst], writes=[b_PB[bk]])

            def e_op(pa_):
                c0, i = pa_[0]["c0"], pa_[0]["i"]
                pp = (i // 2) % 2
                P.op("scalar", lambda e: e.activation(out=e2[:, :, c0:T], in_=PBt[:, 2 * pp:2 * pp + 2, c0:T], func=AF.Exp), reads=[b_PB[2 * pp], b_PB[2 * pp + 1]], writes=[b_e[0]])

            def l_op(pa_):
                c0, i = pa_[0]["c0"], pa_[0]["i"]
                pp = (i // 2) % 2
                P.op("scalar", lambda e: e.activation(out=sp2[pp][:, :, c0:T], in_=e2[:, :, c0:T], func=AF.Ln, bias=1.0), reads=[b_e[0]], writes=[b_sp[pp]])

            def tri(s_):
                c0, i, q = s_["c0"], s_["i"], s_["q"]
                bk, pp = i % NPB, (i // 2) % 2
                P.op("tensor", lambda e: e.matmul(PB[bk][:, c0:T], NTRI, sp2[pp][:, q, c0:T], start=False, stop=s_["first"], skip_group_check=True),
                     reads=[b_sp[pp], b_cst], writes=[b_PB[bk]])
                if not s_["first"]:
                    P.op("tensor", lambda e: e.matmul(PB[bk][:, c0:T], nonesb[:], Srun[q][:, c0:T], start=False, stop=True, skip_group_check=True),
                         reads=[b_S[q], b_cst], writes=[b_PB[bk]])

            def supd(s_):
                c0, i, q = s_["c0"], s_["i"], s_["q"]
                pp = (i // 2) % 2
                if not s_["last"]:
                    if s_["first"]:
                        P.op("vector", lambda e: e.tensor_copy(out=Srun[q][:], in_=sp2[pp][:, q, :]), reads=[b_sp[pp]], writes=[b_S[q]])
                    else:
                        P.op("vector", lambda e: e.tensor_tensor(out=Srun[q][:, c0:T], in0=Srun[q][:, c0:T], in1=sp2[pp][:, q, c0:T], op=ALU.add), reads=[b_sp[pp], b_S[q]], writes=[b_S[q]])

            def xw(pa_):
                c0, i = pa_[0]["c0"], pa_[0]["i"]
                pp = (i // 2) % 2
                P.op("scalar", lambda e: e.activation(out=w2[pp][:, :, c0:T], in_=PBt[:, 2 * pp:2 * pp + 2, c0:T], func=AF.Exp), reads=[b_PB[2 * pp], b_PB[2 * pp + 1]], writes=[b_w[pp]])

            def wv(s_):
                h, kb, c0, i, q = s_["h"], s_["kb"], s_["c0"], s_["i"], s_["q"]
                j, r0 = h // 2, (h % 2) * 64
                pp = (i // 2) % 2
                po = j % NPO
                P.op("tensor", lambda e: e.matmul(PO[po][r0:r0 + 64, c0:T], Vt[:, kb, h * 64:(h + 1) * 64], w2[pp][:, q, c0:T], start=s_["first"], stop=s_["last"], skip_group_check=True),
                     reads=[b_V[kb // 4], b_w[pp]], writes=[b_PO[po]])
                if s_["last"] and h % 2 == 1:
                    evac_copy(yc[:, 4 + j, :], PO[po][:], [b_PO[po]], [b_yc[4 + j]], eng="vector")

            for it in range(npairs + 2):
                P.tile = tt
                if it < npairs:
                    pa_ = pairs[it]
                    qk(pa_[0]); qk(pa_[1])
                    mask(pa_[0]); mask(pa_[1])
                    e_op(pa_)
                if 0 <= it - 1 < npairs:
                    pa_ = pairs[it - 1]
                    tri(pa_[0]); tri(pa_[1])
                    supd(pa_[0]); supd(pa_[1])
                    xw(pa_)
                if it < npairs:
                    l_op(pairs[it])
                if 0 <= it - 2 < npairs:
                    pa_ = pairs[it - 2]
                    wv(pa_[0]); wv(pa_[1])
                yield

        def gen_C(tt, pro=None):
            t0 = tt * T
            yc = YC[tt % 2]
            b_yc = b_YC[tt % 2]

            def st():
                P.tile = tt

            st()
            yield from load_x(tt)

            def ev_res(sl, m, bank):
                st()
                c = sl * 4 + m
                P.op("vector", lambda e: e.tensor_tensor(out=xT[:, c, :], in0=xT[:, c, :], in1=PG[bank][:], op=ALU.add), reads=[b_PG[bank], b_xT[c]], writes=[b_xT[c]])

            st()
            yield from proj_fm("w_out", 2, lambda kc: yc[:, kc, :], b_yc, ev_res)

            def adv_pro():
                if pro is not None:
                    next(pro, None)

            def mlp(grow, last=False):
                st()
                rms_norm()
                norm_apply_h(grow)
                yield
                for half in range(2):
                    def ev_w1(sl, m, bank):
                        st()
                        k = sl * 4 + m
                        q = k % 2
                        if k % 2 == 0:
                            P.op("scalar", lambda e: e.activation(out=ce_b[q][:], in_=PG[bank][:], func=AF.Square), reads=[b_PG[bank]], writes=[b_ce[q]])
                            P.op("vector", lambda e: e.scalar_tensor_tensor(out=aT_chunk(k), in0=PG[bank][:], scalar=0.0, in1=ce_b[q][:], op0=ALU.is_gt, op1=ALU.mult),
                                 reads=[b_PG[bank], b_ce[q]], writes=[b_RA[k // 2]])
                        else:
                            P.op("vector", lambda e: e.tensor_scalar(out=ce_b[q][:], in0=PG[bank][:], scalar1=0.0, scalar2=None, op0=ALU.max), reads=[b_PG[bank]], writes=[b_ce[q]])
                            P.op("vector", lambda e: e.tensor_tensor(out=aT_chunk(k), in0=ce_b[q][:], in1=ce_b[q][:], op=ALU.mult), reads=[b_ce[q]], writes=[b_RA[k // 2]])
                    st()
                    yield from proj_fm("w1", 4, lambda kc: hT[:, kc, :], b_hT, ev_w1)
                    for g in range(2):
                        slots = [next_slab("w2"), next_slab("w2")]
                        for sub in range(2):
                            banks = [pgbank(), pgbank()]
                            for ks in range(2):
                                for mm in range(2):
                                    st()
                                    m = sub * 2 + mm
                                    for kc in range(8):
                                        k = ks * 8 + kc
                                        P.op("tensor", lambda e, slot=slots[ks], m=m, kc=kc, k=k, ks=ks, bank=banks[mm]: e.matmul(PG[bank][:], ring[slot][:, kc, m * 128:(m + 1) * 128], aT_chunk(k), start=(ks == 0 and kc == 0), stop=(ks == 1 and kc == 7)),
                                             reads=[b_ring[slots[ks]], b_RA[k // 2]], writes=[b_PG[banks[mm]]])
                                    if last and half == 1:
                                        pro_bank["forced"] = 3 - banks[0] - banks[1]
                                        adv_pro()
                                        pro_bank["forced"] = None
                                    yield
                            for mm in range(2):
                                st()
                                c = g * 4 + sub * 2 + mm
                                P.op("vector", lambda e, c=c, bank=banks[mm]: e.tensor_tensor(out=xT[:, c, :], in0=xT[:, c, :], in1=PG[bank][:], op=ALU.add), reads=[b_PG[banks[mm]], b_xT[c]], writes=[b_xT[c]])
                        release_slab()
                        release_slab()

            yield from mlp(R_MG0)

            st()
            rms_norm()
            norm_apply_h(R_ODG)
            P.op("vector", lambda e: e.tensor_copy(out=YU[:, :, 0:30], in_=UH[:]), reads=[b_UH], writes=b_YU)
            yield

            def ev_pw1(sl, m, bank):
                st()
                c = (sl % 2) * 4 + m
                if sl < 2:
                    P.op("vector", lambda e: e.tensor_scalar(out=aT_chunk(c), in0=PG[bank][:], scalar1=PAT[:, c, R_PW1B0:R_PW1B0 + 1], scalar2=None, op0=ALU.add),
                         reads=[b_PG[bank], b_par], writes=[b_RA[c // 2]])
                else:
                    q = c % 2
                    P.op("scalar", lambda e: e.activation(out=ce_b[q][:], in_=PG[bank][:], func=AF.Exp, scale=-1.0, bias=DER[:, 16 + c:17 + c]), reads=[b_PG[bank], b_par], writes=[b_ce[q]])
                    P.op("scalar", lambda e: e.activation(out=ce_b[q][:], in_=ce_b[q][:], func=AF.Ln, bias=1.0), reads=[b_ce[q]], writes=[b_ce[q]])
                    P.op("scalar", lambda e: e.activation(out=ce_b[q][:], in_=ce_b[q][:], func=AF.Exp, scale=-1.0), reads=[b_ce[q]], writes=[b_ce[q]])
                    P.op("vector", lambda e: e.tensor_tensor(out=YU[:, c, 30:T + 30], in0=aT_chunk(c), in1=ce_b[q][:], op=ALU.mult), reads=[b_ce[q], b_RA[c // 2]], writes=[b_YU[c]])

            yield from proj_fm("pw1", 4, lambda kc: hT[:, kc, :], b_hT, ev_pw1)
            st()
            P.op("vector", lambda e: e.tensor_copy(out=UH[:], in_=YU[:, :, T:T + 30]), reads=b_YU, writes=[b_UH])

            dgi = {"n": 0}
            for c in range(8):
                bank = pgbank()
                for k in range(31):
                    st()
                    di = dgi["n"] % NDG
                    dgi["n"] += 1
                    P.op("vector", lambda e, c=c, k=k, di=di: e.tensor_scalar(out=dg31[:, di, :], in0=IDB, scalar1=PAT[:, c, R_DWW + k:R_DWW + k + 1], scalar2=None, op0=ALU.mult),
                         reads=[b_par, b_cst], writes=[b_dg[di]])
                    P.op("tensor", lambda e, c=c, k=k, di=di, bank=bank: e.matmul(PG[bank][:], dg31[:, di, :], YU[:, c, k:k + T], start=(k == 0), stop=(k == 30)),
                         reads=[b_dg[di], b_YU[c]], writes=[b_PG[bank]])
                    if k % 8 == 7:
                        yield
                st()
                P.op("scalar", lambda e, c=c, bank=bank: e.activation(out=RA[:, c, :], in_=PG[bank][:], func=AF.Identity, bias=PAT[:, c, R_DWB:R_DWB + 1]), reads=[b_PG[bank], b_par], writes=[b_RA[c]])
                yield

            st()
            bsum = pgbank()
            for c in range(8):
                q = c % 2
                P.op("vector", lambda e, c=c, q=q: e.tensor_copy(out=sq_b[q][:], in_=RA[:, c, :]), reads=[b_RA[c]], writes=[b_sq[q]])
                P.op("tensor", lambda e, c=c, q=q, bsum=bsum: e.matmul(PG[bsum][:], onesb[:], sq_b[q][:], start=(c == 0), stop=(c == 7)), reads=[b_sq[q], b_cst], writes=[b_PG[bsum]])
            yield
            st()
            bsq = pgbank()
            for c in range(8):
                q = c % 2
                P.op("scalar", lambda e, c=c, q=q: e.activation(out=cw_b[q][:], in_=RA[:, c, :], func=AF.Square), reads=[b_RA[c]], writes=[b_cw[q]])
                P.op("tensor", lambda e, c=c, q=q, bsq=bsq: e.matmul(PG[bsq][:], onesb[:], cw_b[q][:], start=(c == 0), stop=(c == 7)), reads=[b_cw[q], b_cst], writes=[b_PG[bsq]])
            yield
            st()
            P.op("vector", lambda e, bsum=bsum: e.tensor_scalar(out=lnv[:], in0=PG[bsum][:], scalar1=1.0 / D, scalar2=None, op0=ALU.mult), reads=[b_PG[bsum]], writes=[b_lnv])
            P.op("vector", lambda e: e.tensor_tensor(out=ce_b[0][:], in0=lnv[:], in1=lnv[:], op=ALU.mult), reads=[b_lnv], writes=[b_ce[0]])
            P.op("vector", lambda e, bsq=bsq: e.scalar_tensor_tensor(out=ce_b[0][:], in0=PG[bsq][:], scalar=1.0 / D, in1=ce_b[0][:], op0=ALU.mult, op1=ALU.subtract), reads=[b_PG[bsq], b_ce[0]], writes=[b_ce[0]])
            P.op("scalar", lambda e: e.activation(out=ce_b[0][:], in_=ce_b[0][:], func=AF.Ln, bias=1e-5), reads=[b_ce[0]], writes=[b_ce[0]])
            P.op("scalar", lambda e: e.activation(out=rstd[:], in_=ce_b[0][:], func=AF.Exp, scale=-0.5), reads=[b_ce[0]], writes=[b_rstd])
            for c in range(8):
                st()
                q = c % 2
                P.op("vector", lambda e, c=c: e.tensor_tensor(out=RA[:, c, :], in0=RA[:, c, :], in1=lnv[:], op=ALU.subtract), reads=[b_RA[c], b_lnv], writes=[b_RA[c]])
                P.op("vector", lambda e, c=c: e.tensor_tensor(out=RA[:, c, :], in0=RA[:, c, :], in1=rstd[:], op=ALU.mult), reads=[b_RA[c], b_rstd], writes=[b_RA[c]])
                P.op("scalar", lambda e, c=c, q=q: e.activation(out=ce_b[q][:], in_=RA[:, c, :], func=AF.Exp, scale=DER[:, 24 + c:25 + c], bias=DER[:, 32 + c:33 + c]), reads=[b_RA[c], b_par], writes=[b_ce[q]])
                P.op("scalar", lambda e, q=q: e.activation(out=ce_b[q][:], in_=ce_b[q][:], func=AF.Ln, bias=1.0), reads=[b_ce[q]], writes=[b_ce[q]])
                P.op("scalar", lambda e, q=q: e.activation(out=ce_b[q][:], in_=ce_b[q][:], func=AF.Exp, scale=-1.0), reads=[b_ce[q]], writes=[b_ce[q]])
                P.op("vector", lambda e, c=c: e.tensor_scalar(out=RA[:, c, :], in0=RA[:, c, :], scalar1=PAT[:, c, R_LNG:R_LNG + 1], scalar2=PAT[:, c, R_LNB:R_LNB + 1], op0=ALU.mult, op1=ALU.add),
                     reads=[b_RA[c], b_par], writes=[b_RA[c]])
                P.op("vector", lambda e, c=c, q=q: e.tensor_tensor(out=hT[:, c, :], in0=RA[:, c, :], in1=ce_b[q][:], op=ALU.mult), reads=[b_RA[c], b_ce[q]], writes=[b_hT[c]])
                yield

            def ev_pw2(sl, m, bank):
                st()
                c = sl * 4 + m
                P.op("vector", lambda e: e.scalar_tensor_tensor(out=xT[:, c, :], in0=PG[bank][:], scalar=PAT[:, c, R_PW2B:R_PW2B + 1], in1=xT[:, c, :], op0=ALU.add, op1=ALU.add),
                     reads=[b_PG[bank], b_xT[c], b_par], writes=[b_xT[c]])

            yield from proj_fm("pw2", 2, lambda kc: hT[:, kc, :], b_hT, ev_pw2)

            yield from mlp(R_MG1, last=True)

            st()
            rms_norm()
            for c in range(8):
                P.op("vector", lambda e, c=c: e.scalar_tensor_tensor(out=RA[:, c, :], in0=xT[:, c, :], scalar=PAT[:, c, R_FG:R_FG + 1], in1=rstd[:], op0=ALU.mult, op1=ALU.mult),
                     reads=[b_xT[c], b_rstd, b_par], writes=[b_RA[c]])
            yield
            for s in range(4):
                st()
                oi = s % 2
                for half in range(2):
                    bank = pgbank()
                    for cc in range(4):
                        c = half * 4 + cc
                        P.op("tensor", lambda e, c=c, cc=cc, s=s, bank=bank: e.transpose(out=PG[bank][:, cc * 128:(cc + 1) * 128], in_=RA[:, c, s * 128:(s + 1) * 128], identity=IDF),
                             reads=[b_RA[c], b_cst], writes=[b_PG[bank]])
                    evac_copy(ost[oi][:, half * 512:(half + 1) * 512], PG[bank][:], [b_PG[bank]], [b_ost[oi]])
                dma("gpsimd", y_d[t0 + s * 128:t0 + (s + 1) * 128, :], ost[oi][:], ost_sem[oi], reads=[b_ost[oi]])
                adv_pro()
                yield
            if pro is not None:
                run(pro)

        NC_EST = 214.0

        def interleave(gb, gc, nb):
            import os
            if os.environ.get("KSEQ") == "1":
                run(gb)
                run(gc)
                return
            if os.environ.get("KSEQ") == "2":
                run(gc)
                run(gb)
                return
            ratio = NC_EST / max(nb, 1)
            acc = 0.0
            c_done = False
            for _ in range(nb):
                next(gb, None)
                acc += ratio
                while acc >= 1.0 and not c_done:
                    acc -= 1.0
                    try:
                        next(gc)
                    except StopIteration:
                        c_done = True
            run(gb)
            if not c_done:
                run(gc)

        stage_A(0)
        run(gen_B(0))
        for tt in range(NT):
            if tt + 1 < NT:
                stage_A(tt + 1, pro_done=(tt >= 1))
                nb = 4 * (4 * (tt + 1) + 4) + 2
                pro = gen_A_pro(tt + 2) if tt + 2 < NT else None
                interleave(gen_B(tt + 1), gen_C(tt, pro), nb)
            else:
                run(gen_C(tt))

        P.tile = NT
        last = [o for o in P.ops["gpsimd"] if o.dma and o.sem in ost_sem]
        P.op("gpsimd", lambda e: e.nop(), extra=last[-2:])

        P.finalize(engsems)
        blk = es.enter_context(nc.Block())

        @blk.sync
        def _(e):
            P.emit("sync", e)

        @blk.scalar
        def _(e):
            P.emit("scalar", e)

        @blk.gpsimd
        def _(e):
            P.emit("gpsimd", e)

        @blk.vector
        def _(e):
            P.emit("vector", e)

        @blk.tensor
        def _(e):
            P.emit("tensor", e)

    return nc


def make_consts():
    c = np.zeros((128, 896), np.float32)
    c[:, 0:128] = np.eye(128, dtype=np.float32)
    j = np.arange(128)[:, None]
    s = np.arange(128)[None, :]
    c[:, 128:256] = np.where(j >= s, -1.0, 0.0)
    c[:, 256:384] = np.where(j >= s, NEG, 0.0)
    c[:, 384:768] = NEG
    c[:, 768:896] = c[:, 256:384]
    return c


def pack_inputs(inp, NT=8):
    f = lambda a: np.ascontiguousarray(np.asarray(a, dtype=np.float32))
    pa = np.zeros((NPA, D), np.float32)
    pa[R_EVG] = inp["ev_norm_g"][0]
    pa[R_ODG] = inp["od_norm_g"][0]
    pa[R_MG0] = inp["mlp_norm_g"][0]
    pa[R_MG1] = inp["mlp_norm_g"][1]
    pa[R_FG] = inp["final_g"]
    pa[R_DWB] = inp["od_dw_b"][0]
    pa[R_LNG] = inp["od_ln_g"][0]
    pa[R_LNB] = inp["od_ln_b"][0]
    pa[R_PW2B] = inp["od_pw2_b"][0]
    pa[R_PW1B0] = inp["od_pw1_b"][0][:D]
    pa[R_PW1B1] = inp["od_pw1_b"][0][D:]
    pa[R_DWW:R_DWW + 31] = inp["od_dw_w"][0]
    pb = np.zeros((8, 512), np.float32)
    pb[0:4] = inp["ev_conv_w"][0]
    pb[4] = inp["ev_conv_b"][0]
    pb[5] = inp["ev_gate_a_b"][0]
    pb[6] = inp["ev_gate_x_b"][0]
    pb[7] = inp["ev_lam"][0]
    shared = {
        "w_in": f(inp["ev_w_in"][0]), "w_out": f(inp["ev_w_out"][0]),
        "w1": f(inp["mlp_w1"]), "w2": f(inp["mlp_w2"]),
        "pw1": f(inp["od_pw1_w"][0]), "pw2": f(inp["od_pw2_w"][0]),
        "pa": pa, "pb": pb, "gaw": f(inp["ev_gate_a_w"][0]), "gxw": f(inp["ev_gate_x_w"][0]),
        "cst": make_consts(),
    }
    x = f(inp["x"])
    return [dict(shared, x=x[b]) for b in range(x.shape[0])]


_NC_CACHE = {}


def kernel(**inputs):
    inputs = {k: np.asarray(v) for k, v in inputs.items()}
    if "nc" not in _NC_CACHE:
        _NC_CACHE["nc"] = build(8)
    nc = _NC_CACHE["nc"]
    in_maps = pack_inputs(inputs)
    res = run_bass_kernel_spmd(nc, in_maps, core_ids=list(range(8)))
    return np.stack([np.asarray(r["y"], dtype=np.float32) for r in res.results], axis=0)
```

```python
from contextlib import ExitStack

import numpy as np
import concourse.bass as bass
import concourse.mybir as mybir
from concourse.bass_utils import run_bass_kernel_spmd

F32 = mybir.dt.float32
BF16 = mybir.dt.bfloat16
AF = mybir.ActivationFunctionType
ALU = mybir.AluOpType

D = 1024
S = 4096
T = 512
NSLOT = 3
NEG = -30000.0
ENGS = ("tensor", "vector", "scalar", "gpsimd", "sync")

R_EVG, R_ODG, R_MG0, R_MG1, R_FG, R_DWB, R_LNG, R_LNB, R_PW2B, R_PW1B0, R_PW1B1, R_DWW = 0, 1, 2, 3, 4, 5, 6, 7, 8, 9, 10, 11
NPA = 42


class Sem:
    def __init__(self, h):
        self.h = h
        self.count = 0


class Op:
    __slots__ = ("eng", "fn", "deps", "sem", "val", "dma", "used", "tile")

    def __init__(self, eng, fn, deps, dma_sem, tile):
        self.eng = eng
        self.fn = fn
        self.deps = deps
        self.sem = dma_sem
        self.val = None
        self.dma = dma_sem is not None
        self.used = False
        self.tile = tile


class Buf:
    __slots__ = ("name", "w", "r")

    def __init__(self, name):
        self.name = name
        self.w = None
        self.r = []


class Prog:
    def __init__(self):
        self.ops = {e: [] for e in ENGS}
        self.tile = 0

    def op(self, eng, fn, reads=(), writes=(), dma_sem=None, extra=(), nochain=False):
        deps = set(extra)
        for b in reads:
            if b.w is not None:
                deps.add(b.w)
        for b in writes:
            if b.w is not None:
                deps.add(b.w)
            deps.update(b.r)
        o = Op(eng, fn, deps, dma_sem, self.tile)
        deps.discard(o)
        if dma_sem is not None:
            deps = {d for d in deps if not (d.dma and d.sem is dma_sem and d.eng == eng)} if nochain else deps
            o.deps = deps
        for b in reads:
            b.r.append(o)
        for b in writes:
            b.w = o
            b.r = []
        self.ops[eng].append(o)
        return o

    def finalize(self, engsems):
        for e in ENGS:
            for o in self.ops[e]:
                for d in o.deps:
                    if d.eng == "tensor" and o.eng == "tensor":
                        continue
                    d.used = True
        for e in ENGS:
            for o in self.ops[e]:
                if o.dma:
                    o.sem.count += 16
                    o.val = o.sem.count
                elif o.used:
                    s = engsems[e][o.tile]
                    s.count += 1
                    o.sem = s
                    o.val = s.count

    def emit(self, eng_name, eng):
        waited = {}
        for o in self.ops[eng_name]:
            need = {}
            for d in o.deps:
                if d.eng == "tensor" and o.eng == "tensor":
                    continue
                if d.sem is None:
                    continue
                if need.get(d.sem, 0) < d.val:
                    need[d.sem] = d.val
            for s, v in need.items():
                if waited.get(s, 0) < v:
                    eng.wait_ge(s.h, v)
                    waited[s] = v
            ins = o.fn(eng)
            if o.sem is not None and (o.dma or o.used):
                ins.then_inc(o.sem.h, 16 if o.dma else 1)


def slab_table():
    t = []
    for nb in range(5):
        t.append(("w_in", 0, 0, nb))
    for nb in range(2):
        t.append(("w_out", 0, 0, nb))

    def mlp(l):
        for half in range(2):
            for nb in range(4):
                t.append(("w1", l, 0, half * 4 + nb))
            for q4 in range(4):
                t.append(("w2", l, half, q4))

    mlp(0)
    for nb in range(4):
        t.append(("pw1", 0, 0, nb))
    for nb in range(2):
        t.append(("pw2", 0, 0, nb))
    mlp(1)
    return t


SLABS = slab_table()
NSLAB = len(SLABS)


def build(NT=8, dbg=None):
    nc = bass.Bass("TRN2", target_bir_lowering=False)
    P = Prog()
    ntok = NT * T

    def din(name, shape):
        return nc.dram_tensor(name, shape, F32, kind="ExternalInput").ap()

    x_d = din("x", [S, D])
    w_d = {
        "w_in": din("w_in", [D, 2560]),
        "w_out": din("w_out", [D, D]),
        "w1": din("w1", [2, D, 4 * D]),
        "w2": din("w2", [2, 4 * D, D]),
        "pw1": din("pw1", [D, 2 * D]),
        "pw2": din("pw2", [D, D]),
    }
    pa_d = din("pa", [NPA, D])
    pb_d = din("pb", [8, 512])
    gaw_d = din("gaw", [8, 64, 64])
    gxw_d = din("gxw", [8, 64, 64])
    cst_d = din("cst", [128, 896])
    y_d = nc.dram_tensor("y", [S, D], F32, kind="ExternalOutput").ap()
    wscr = nc.dram_tensor("wscr", [NSLAB, 128, 4096], BF16, kind="Internal").ap()
    dbg_out = {}

    es = ExitStack()
    with es:
        def sb(name, shape, dt=F32):
            return es.enter_context(nc.sbuf_tensor(name, shape, dt))

        def newsem(name):
            return Sem(es.enter_context(nc.semaphore(name)))

        KT = sb("KT", [128, 4, S], BF16)
        Vt = sb("Vt", [128, 32, 512], BF16)
        xT = sb("xT", [128, 8, T])
        io_b = [sb(f"io{i}", [128, D]) for i in range(2)]
        xin = io_b
        ost = io_b
        hT = sb("hT", [128, 8, T], BF16)
        RA = sb("RA", [128, 8, T])
        XRb = sb("XRb", [128, 4, T + 3], BF16)
        QT = sb("QT", [128, 4, T], BF16)
        YC = [sb(f"YC{i}", [128, 8, T], BF16) for i in range(2)]
        YU = sb("YU", [128, 8, T + 30], BF16)
        ce_b = [sb(f"ce{i}", [128, T]) for i in range(2)]
        cw_b = [sb(f"cw{i}", [128, T], BF16) for i in range(2)]
        UH = sb("UH", [128, 8, 30], BF16)
        e2 = sb("e2", [128, 2, T])
        e_b = [e2[:, i, :] for i in range(2)]
        sp2 = [sb(f"sp2_{i}", [128, 2, T], BF16) for i in range(2)]
        w2 = [sb(f"w2_{i}", [128, 2, T], BF16) for i in range(2)]
        Srun = [sb(f"Srun{i}", [128, T], BF16) for i in range(2)]
        sq_b = [sb(f"sq{i}", [128, T], BF16) for i in range(2)]
        rstd = sb("rstd", [128, T])
        lnv = sb("lnv", [128, T])
        xcb = sq_b[0]
        ring = [sb(f"ring{i}", [128, 8, 512], BF16) for i in range(NSLOT)]
        cstf = sb("cstf", [128, 128])
        cstb = sb("cstb", [128, 896], BF16)
        onesb = sb("onesb", [128, 128], BF16)
        nonesb = sb("nonesb", [128, 128], BF16)
        PAT = sb("PAT", [128, 8, NPA])
        PBT = sb("PBT", [128, 4, 8])
        DER = sb("DER", [128, 48])
        BDb = sb("BDb", [128, 4, 2, 128], BF16)
        DG4 = sb("DG4", [128, 4, 4, 128], BF16)
        NDG = 4
        dg31 = sb("dg31", [128, NDG, 128], BF16)
        hcar = sb("hcar", [128, 4])

        pas = RA[0:NPA, 0:2, :].rearrange("p a b -> p (a b)")
        pbs = RA[0:8, 2, :]
        BDf = RA[:, 3:5, :].rearrange("p a b -> p (a b)").rearrange("p (a b c) -> p a b c", a=4, b=2)
        xstA = [hT[:, 4 * i:4 * i + 4, :].rearrange("p a b -> p (a b)").bitcast(F32) for i in range(2)]
        IDF = cstf[:, 0:128]
        IDB = cstb[:, 0:128]
        NTRI = cstb[:, 128:256]
        NEGTRI = cstb[:, 256:384]
        NEGF3 = cstb[:, 384:896]


        def aT_chunk(k):
            v = RA[:, k // 2, :].bitcast(BF16)
            return v[:, (k % 2) * T:(k % 2 + 1) * T]

        NPB = 4
        NPG = 3
        NPO = 1
        PBt = es.enter_context(nc.psum_tensor("pbt", [128, NPB, T], F32))
        PB = [PBt[:, i, :] for i in range(NPB)]
        PO = [es.enter_context(nc.psum_tensor(f"po{i}", [128, T], F32)) for i in range(NPO)]
        PG = [es.enter_context(nc.psum_tensor(f"pg{i}", [128, T], F32)) for i in range(NPG)]

        engsems = {e: [newsem(f"s_{e}_{t}") for t in range(NT + 1)] for e in ("tensor", "vector", "scalar", "gpsimd")}
        engsems["sync"] = [None] * (NT + 1)
        ring_sem = [newsem(f"ring{i}") for i in range(NSLOT)]
        cast_sem = [newsem(f"cast{i}") for i in range(4)]
        xin_sem = [newsem(f"xin{i}") for i in range(2)]
        xa_sem = [newsem(f"xa{i}") for i in range(2)]
        ost_sem = [newsem(f"ost{i}") for i in range(2)]
        setup_sem = newsem("setup")
        setup_sem2 = newsem("setup2")

        def B(name):
            return Buf(name)

        b_KT = [[B(f"KT{j}_{t}") for t in range(NT)] for j in range(4)]
        b_V = [B(f"V{t}") for t in range(NT)]
        b_xT = [B(f"xT{c}") for c in range(8)]
        b_xin = [B("io0"), B("io1")]
        b_ost = b_xin
        b_hT = [B(f"hT{c}") for c in range(8)]
        b_RA = [B(f"RA{c}") for c in range(8)]
        b_XR = [B(f"XR{m}") for m in range(4)]
        b_XRh = B("XRh")
        b_QT = [B(f"QT{m}") for m in range(4)]
        b_YU = [B(f"YU{c}") for c in range(8)]
        b_YC = [[B(f"YC{i}_{c}") for c in range(8)] for i in range(2)]
        b_ce = [B("ce0"), B("ce1")]
        b_cw = [B("cw0"), B("cw1")]
        b_UH = B("UH")
        b_e = [B("e2")]
        b_sp = [B(f"sp2_{i}") for i in range(2)]
        b_w = [B(f"w2_{i}") for i in range(2)]
        b_S = [B("S0"), B("S1")]
        b_sq = [B("sq0"), B("sq1")]
        b_rstd = B("rstd")
        b_lnv = B("lnv")
        b_xcb = b_sq[0]
        b_ring = [B(f"ring{i}") for i in range(NSLOT)]
        b_PB = [B(f"PB{i}") for i in range(NPB)]
        b_PO = [B(f"PO{i}") for i in range(NPO)]
        b_PG = [B(f"PG{i}") for i in range(NPG)]
        b_wscr = [B(f"wscr{i}") for i in range(NSLAB)]
        b_cst = B("cst")
        b_par = B("params")
        b_dg = [B(f"dg{i}") for i in range(NDG)]
        b_hcar = B("hcar")
        b_stg = b_RA[0:5]

        def dma(q, out, in_, sem, reads=(), writes=(), nochain=False):
            return P.op(q, lambda e, out=out, in_=in_: e.dma_start(out=out, in_=in_), reads=reads, writes=writes, dma_sem=sem, nochain=nochain)

        dma("gpsimd", cstb[:], cst_d[:, :], setup_sem2, writes=[b_cst])
        cast_ops = []
        for i, (key, l, ks, nb) in enumerate(SLABS):
            W = w_d[key]
            W2 = W[l] if key in ("w1", "w2") else W
            if key == "w2":
                src = W2[ks * 2048:(ks + 1) * 2048, nb * 256:(nb + 1) * 256].rearrange("(kc p) c -> p kc c", p=128)
                dst = wscr[i].rearrange("p (kc c) -> p kc c", c=256)
            else:
                src = W2[ks * 1024:(ks + 1) * 1024, nb * 512:(nb + 1) * 512].rearrange("(kc p) c -> p kc c", p=128)
                dst = wscr[i].rearrange("p (kc c) -> p kc c", c=512)
            cast_ops.append(P.op("gpsimd", lambda e, dst=dst, src=src: e.dma_start(out=dst, in_=src), writes=[b_wscr[i]], dma_sem=cast_sem[i % 4],
                                 extra=(cast_ops[i - 3:i - 2] if i >= 3 else ())))

        dma("sync", cstf[:], cst_d[:, 0:128], setup_sem, writes=[b_cst], nochain=True)
        dma("sync", pas[:], pa_d[:, :], setup_sem, writes=[b_par] + b_stg, nochain=True)
        dma("sync", pbs[:], pb_d[:, :], setup_sem, writes=[b_par] + b_stg, nochain=True)
        P.op("vector", lambda e: e.memset(BDf.rearrange("p a b c -> p (a b c)"), 0.0), writes=[b_par] + b_stg)
        for h in range(8):
            r0 = (h % 2) * 64
            dma("sync", BDf[r0:r0 + 64, h // 2, 0, r0:r0 + 64], gaw_d[h], setup_sem, writes=[b_par] + b_stg, nochain=True)
            dma("sync", BDf[r0:r0 + 64, h // 2, 1, r0:r0 + 64], gxw_d[h], setup_sem, writes=[b_par] + b_stg, nochain=True)

        P.op("vector", lambda e: e.memset(onesb[:], 1.0), writes=[b_cst])
        P.op("vector", lambda e: e.memset(nonesb[:], -1.0), writes=[b_cst])
        P.op("vector", lambda e: e.memset(hcar[:], 0.0), writes=[b_hcar])
        P.op("vector", lambda e: e.memset(XRb[:, :, 0:3], 0.0), writes=[b_XRh])
        P.op("vector", lambda e: e.memset(UH[:].rearrange("p a b -> p (a b)"), 0.0), writes=[b_UH])
        P.op("vector", lambda e: e.tensor_copy(out=BDb[:].rearrange("p a b c -> p (a b c)"), in_=BDf.rearrange("p a b c -> p (a b c)")), reads=[b_par] + b_stg, writes=[b_par])

        for c in range(8):
            P.op("tensor", lambda e, c=c: e.transpose(out=PG[0][:, c * NPA:(c + 1) * NPA], in_=pas[:, c * 128:(c + 1) * 128], identity=IDF[0:NPA, 0:NPA]),
                 reads=[b_par, b_cst] + b_stg, writes=[b_PG[0]])
        P.op("vector", lambda e: e.tensor_copy(out=PAT[:].rearrange("p c r -> p (c r)"), in_=PG[0][:, 0:8 * NPA]), reads=[b_PG[0]], writes=[b_par])
        for c in range(4):
            P.op("tensor", lambda e, c=c: e.transpose(out=PG[1][:, c * 8:(c + 1) * 8], in_=pbs[:, c * 128:(c + 1) * 128], identity=IDF[0:8, 0:8]),
                 reads=[b_par, b_cst] + b_stg, writes=[b_PG[1]])
        P.op("vector", lambda e: e.tensor_copy(out=PBT[:].rearrange("p c r -> p (c r)"), in_=PG[1][:, 0:32]), reads=[b_PG[1]], writes=[b_par])

        P.op("vector", lambda e: e.tensor_scalar(out=DER[:, 0:4], in0=PBT[:, :, 5], scalar1=-1.0, scalar2=None, op0=ALU.mult), reads=[b_par], writes=[b_par])
        P.op("vector", lambda e: e.tensor_scalar(out=DER[:, 4:8], in0=PBT[:, :, 6], scalar1=-1.0, scalar2=None, op0=ALU.mult), reads=[b_par], writes=[b_par])
        P.op("scalar", lambda e: e.activation(out=DER[:, 40:44], in_=PBT[:, :, 7], func=AF.Exp, scale=-1.0), reads=[b_par], writes=[b_par])
        P.op("scalar", lambda e: e.activation(out=DER[:, 44:48], in_=DER[:, 40:44], func=AF.Ln, bias=1.0), reads=[b_par], writes=[b_par])
        P.op("vector", lambda e: e.tensor_scalar(out=DER[:, 8:12], in0=DER[:, 44:48], scalar1=-8.0, scalar2=None, op0=ALU.mult), reads=[b_par], writes=[b_par])
        P.op("vector", lambda e: e.tensor_scalar(out=DER[:, 12:16], in0=DER[:, 44:48], scalar1=-16.0, scalar2=None, op0=ALU.mult), reads=[b_par], writes=[b_par])
        P.op("vector", lambda e: e.tensor_scalar(out=DER[:, 16:24], in0=PAT[:, :, R_PW1B1], scalar1=-1.0, scalar2=None, op0=ALU.mult), reads=[b_par], writes=[b_par])
        P.op("vector", lambda e: e.tensor_scalar(out=DER[:, 24:32], in0=PAT[:, :, R_LNG], scalar1=-1.0, scalar2=None, op0=ALU.mult), reads=[b_par], writes=[b_par])
        P.op("vector", lambda e: e.tensor_scalar(out=DER[:, 32:40], in0=PAT[:, :, R_LNB], scalar1=-1.0, scalar2=None, op0=ALU.mult), reads=[b_par], writes=[b_par])
        for m in range(4):
            for k in range(4):
                P.op("vector", lambda e, m=m, k=k: e.tensor_scalar(out=DG4[:, m, k, :], in0=IDB, scalar1=PBT[:, m, k:k + 1], scalar2=None, op0=ALU.mult),
                     reads=[b_par, b_cst], writes=[b_par])

        st = {"next_load": 0, "next_use": 0}
        seq = list(range(5))
        for t_ in range(NT):
            if t_ + 1 < NT:
                seq += list(range(5))
            seq += list(range(5, NSLAB))
        total_slabs = len(seq)

        def issue_load(gi):
            slot = gi % NSLOT
            i = seq[gi]
            dma("sync", ring[slot][:].rearrange("p a b -> p (a b)"), wscr[i], ring_sem[slot], reads=[b_wscr[i]], writes=[b_ring[slot]])

        for gi in range(min(NSLOT, total_slabs)):
            issue_load(gi)
        st["next_load"] = min(NSLOT, total_slabs)

        def next_slab(expect):
            gi = st["next_use"]
            assert SLABS[seq[gi]][0] == expect, (SLABS[seq[gi]], expect)
            st["next_use"] += 1
            return gi % NSLOT

        def release_slab():
            if st["next_load"] < total_slabs:
                issue_load(st["next_load"])
                st["next_load"] += 1

        rr = {"pg": 0, "ev": 0}

        def pgbank():
            rr["pg"] = (rr["pg"] + 1) % NPG
            return rr["pg"]

        def evac_copy(out_ap, in_ap, reads, writes, eng=None, scale=None):
            if eng is None:
                rr["ev"] ^= 1
                eng = "vector" if rr["ev"] else "scalar"
            if eng == "vector":
                if scale is None:
                    return P.op("vector", lambda e: e.tensor_copy(out=out_ap, in_=in_ap), reads=reads, writes=writes)
                return P.op("vector", lambda e: e.tensor_scalar(out=out_ap, in0=in_ap, scalar1=scale, scalar2=None, op0=ALU.mult), reads=reads, writes=writes)
            sc = 1.0 if scale is None else scale
            return P.op("scalar", lambda e: e.activation(out=out_ap, in_=in_ap, func=AF.Identity, scale=sc), reads=reads, writes=writes)

        def run(gen):
            for _ in gen:
                pass

        def load_x(tt, in_A=False):
            t0 = tt * T
            for s in range(4):
                bi = s % 2
                if in_A:
                    stg, bst, ssem = xstA[bi], b_hT[4 * bi:4 * bi + 4], xa_sem[bi]
                else:
                    stg, bst, ssem = xin[bi][:], [b_xin[bi]], xin_sem[bi]
                dma("sync", stg, x_d[t0 + s * 128:t0 + (s + 1) * 128, :], ssem, writes=bst)
                for half in range(2):
                    bank = pgbank()
                    for cc in range(4):
                        c = half * 4 + cc
                        P.op("tensor", lambda e, stg=stg, c=c, cc=cc, bank=bank: e.transpose(out=PG[bank][:, cc * 128:(cc + 1) * 128], in_=stg[:, c * 128:(c + 1) * 128], identity=IDF),
                             reads=list(bst) + [b_cst], writes=[b_PG[bank]])
                    P.op("vector", lambda e, half=half, s=s, bank=bank: e.tensor_copy(out=xT[:, half * 4:half * 4 + 4, s * 128:(s + 1) * 128], in_=PG[bank][:].rearrange("p (c t) -> p c t", t=128)),
                         reads=[b_PG[bank]], writes=[b_xT[half * 4 + i] for i in range(4)])
                yield

        def rms_norm():
            bank = pgbank()
            for c in range(8):
                q = c % 2
                P.op("scalar", lambda e, c=c, q=q: e.activation(out=sq_b[q][:], in_=xT[:, c, :], func=AF.Square), reads=[b_xT[c]], writes=[b_sq[q]])
                P.op("tensor", lambda e, c=c, q=q, bank=bank: e.matmul(PG[bank][:], onesb[:], sq_b[q][:], start=(c == 0), stop=(c == 7)),
                     reads=[b_sq[q], b_cst], writes=[b_PG[bank]])
            P.op("scalar", lambda e, bank=bank: e.activation(out=lnv[:], in_=PG[bank][:], func=AF.Ln, scale=1.0 / D, bias=1e-6), reads=[b_PG[bank]], writes=[b_lnv])
            P.op("scalar", lambda e: e.activation(out=rstd[:], in_=lnv[:], func=AF.Exp, scale=-0.5), reads=[b_lnv], writes=[b_rstd])

        def norm_apply_h(grow):
            for c in range(8):
                P.op("vector", lambda e, c=c: e.scalar_tensor_tensor(out=hT[:, c, :], in0=xT[:, c, :], scalar=PAT[:, c, grow:grow + 1], in1=rstd[:], op0=ALU.mult, op1=ALU.mult),
                     reads=[b_xT[c], b_rstd, b_par], writes=[b_hT[c]])

        def proj_fm(key, nslabs, src_chunks, b_src, evac):
            for sl in range(nslabs):
                slot = next_slab(key)
                for m in range(4):
                    bank = pgbank()
                    for kc in range(8):
                        P.op("tensor", lambda e, slot=slot, m=m, kc=kc, bank=bank: e.matmul(PG[bank][:], ring[slot][:, kc, m * 128:(m + 1) * 128], src_chunks(kc), start=(kc == 0), stop=(kc == 7)),
                             reads=[b_ring[slot], b_src[kc]], writes=[b_PG[bank]])
                    evac(sl, m, bank)
                    yield
                release_slab()

        def stage_A(tt):
            P.tile = tt
            t0 = tt * T
            yc = YC[tt % 2]
            b_yc = b_YC[tt % 2]
            run(load_x(tt, in_A=True))
            rms_norm()
            norm_apply_h(R_EVG)
            if tt > 0:
                P.op("vector", lambda e: e.tensor_copy(out=XRb[:, :, 0:3], in_=XRb[:, :, T:T + 3]), reads=b_XR, writes=[b_XRh])

            def ev_in(sl, m, bank):
                if sl == 0:
                    G = RA[:, m, :]
                    bG = b_RA[m]
                    Tg = RA[:, 4 + m, :]
                    bTg = b_RA[4 + m]
                    evac_copy(G, PG[bank][:], [b_PG[bank]], [bG], eng="vector")
                    P.op("scalar", lambda e: e.activation(out=Tg, in_=G, func=AF.Square), reads=[bG], writes=[bTg])
                    P.op("vector", lambda e: e.tensor_scalar(out=Tg, in0=Tg, scalar1=0.044715, scalar2=1.0, op0=ALU.mult, op1=ALU.add), reads=[bTg], writes=[bTg])
                    P.op("vector", lambda e: e.tensor_tensor(out=Tg, in0=Tg, in1=G, op=ALU.mult), reads=[bTg, bG], writes=[bTg])
                    P.op("scalar", lambda e: e.activation(out=Tg, in_=Tg, func=AF.Exp, scale=-1.5957691216057308), reads=[bTg], writes=[bTg])
                    P.op("scalar", lambda e: e.activation(out=Tg, in_=Tg, func=AF.Ln, bias=1.0), reads=[bTg], writes=[bTg])
                    P.op("scalar", lambda e: e.activation(out=Tg, in_=Tg, func=AF.Exp, scale=-1.0), reads=[bTg], writes=[bTg])
                    P.op("vector", lambda e: e.tensor_tensor(out=G, in0=Tg, in1=G, op=ALU.mult), reads=[bTg, bG], writes=[bG])
                elif sl == 1:
                    evac_copy(XRb[:, m, 3:T + 3], PG[bank][:], [b_PG[bank], b_XRh], [b_XR[m]])
                elif sl == 2:
                    evac_copy(QT[:, m, :], PG[bank][:], [b_PG[bank]], [b_QT[m]], eng="vector", scale=0.125)
                else:
                    evac_copy(KT[:, m, t0:t0 + T], PG[bank][:], [b_PG[bank]], [b_KT[m][tt]], eng="vector")

            src_h = lambda kc: hT[:, kc, :]
            run(proj_fm("w_in", 2, src_h, b_hT, ev_in))

            def rest_gen():
                yield from proj_fm("w_in", 2, src_h, b_hT, lambda sl, m, bank: ev_in(sl + 2, m, bank))
                slot = next_slab("w_in")
                for s in range(4):
                    bank = pgbank()
                    for kc in range(8):
                        P.op("tensor", lambda e, slot=slot, s=s, kc=kc, bank=bank: e.matmul(PG[bank][:], hT[:, kc, s * 128:(s + 1) * 128], ring[slot][:, kc, :], start=(kc == 0), stop=(kc == 7)),
                             reads=[b_ring[slot], b_hT[kc]], writes=[b_PG[bank]])
                    evac_copy(Vt[:, tt * 4 + s, :], PG[bank][:], [b_PG[bank]], [b_V[tt]], eng="vector")
                    yield
                release_slab()

            def lru_gen():
                T0, T1, T2, T3 = (RA[:, 4, :], RA[:, 5, :], RA[:, 6, :], RA[:, 7, :])
                bT = b_RA[4:8]
                for m in range(4):
                    G = RA[:, m, :]
                    bG = b_RA[m]
                    bank = pgbank()
                    for k in range(4):
                        P.op("tensor", lambda e, m=m, k=k, bank=bank: e.matmul(PG[bank][:], DG4[:, m, k, :], XRb[:, m, k:k + T], start=(k == 0), stop=(k == 3)),
                             reads=[b_par, b_XR[m], b_XRh], writes=[b_PG[bank]])
                    P.op("scalar", lambda e, m=m, bank=bank: e.activation(out=T0, in_=PG[bank][:], func=AF.Identity, bias=PBT[:, m, 4:5]), reads=[b_PG[bank], b_par], writes=[bT[0]])
                    P.op("vector", lambda e: e.tensor_copy(out=xcb[:], in_=T0), reads=[bT[0]], writes=[b_xcb])
                    yield
                    P.op("tensor", lambda e, m=m: e.matmul(PB[0][:], BDb[:, m, 0, :], xcb[:], start=True, stop=True), reads=[b_par, b_xcb], writes=[b_PB[0]])
                    P.op("tensor", lambda e, m=m: e.matmul(PB[1][:], BDb[:, m, 1, :], xcb[:], start=True, stop=True), reads=[b_par, b_xcb], writes=[b_PB[1]])
                    yield
                    P.op("scalar", lambda e, m=m: e.activation(out=T1, in_=PB[0][:], func=AF.Exp, scale=-1.0, bias=DER[:, m:m + 1]), reads=[b_PB[0], b_par], writes=[bT[1]])
                    P.op("scalar", lambda e: e.activation(out=T1, in_=T1, func=AF.Ln, bias=1.0), reads=[bT[1]], writes=[bT[1]])
                    P.op("scalar", lambda e: e.activation(out=T1, in_=T1, func=AF.Exp, scale=-1.0), reads=[bT[1]], writes=[bT[1]])
                    P.op("scalar", lambda e, m=m: e.activation(out=T2, in_=T1, func=AF.Exp, scale=DER[:, 8 + m:9 + m]), reads=[bT[1], b_par], writes=[bT[2]])
                    P.op("scalar", lambda e, m=m: e.activation(out=T3, in_=T1, func=AF.Exp, scale=DER[:, 12 + m:13 + m]), reads=[bT[1], b_par], writes=[bT[3]])
                    P.op("scalar", lambda e: e.activation(out=T3, in_=T3, func=AF.Ln, scale=-1.0, bias=1.0), reads=[bT[3]], writes=[bT[3]])
                    P.op("scalar", lambda e: e.activation(out=T3, in_=T3, func=AF.Exp, scale=0.5), reads=[bT[3]], writes=[bT[3]])
                    P.op("scalar", lambda e, m=m: e.activation(out=T1, in_=PB[1][:], func=AF.Exp, scale=-1.0, bias=DER[:, 4 + m:5 + m]), reads=[b_PB[1], b_par], writes=[bT[1]])
                    P.op("scalar", lambda e: e.activation(out=T1, in_=T1, func=AF.Ln, bias=1.0), reads=[bT[1]], writes=[bT[1]])
                    P.op("scalar", lambda e: e.activation(out=T1, in_=T1, func=AF.Exp, scale=-1.0), reads=[bT[1]], writes=[bT[1]])
                    P.op("vector", lambda e: e.tensor_tensor(out=T1, in0=T1, in1=T0, op=ALU.mult), reads=[bT[1], bT[0]], writes=[bT[1]])
                    P.op("vector", lambda e: e.tensor_tensor(out=T3, in0=T3, in1=T1, op=ALU.mult), reads=[bT[3], bT[1]], writes=[bT[3]])
                    P.op("vector", lambda e, m=m: e.tensor_tensor_scan(out=T0, data0=T2, data1=T3, initial=hcar[:, m:m + 1], op0=ALU.mult, op1=ALU.add),
                         reads=[bT[2], bT[3], b_hcar], writes=[bT[0]])
                    P.op("vector", lambda e, m=m: e.tensor_copy(out=hcar[:, m:m + 1], in_=T0[:, T - 1:T]), reads=[bT[0]], writes=[b_hcar])
                    P.op("vector", lambda e, m=m, yc=yc, G=G: e.tensor_tensor(out=yc[:, m, :], in0=G, in1=T0, op=ALU.mult), reads=[bG, bT[0]], writes=[b_yc[m]])
                    yield


            g_r, g_l = rest_gen(), lru_gen()
            live = [g_r, g_l]
            while live:
                for g in list(live):
                    try:
                        next(g)
                    except StopIteration:
                        live.remove(g)

        def gen_B(tt):
            yc = YC[tt % 2]
            b_yc = b_YC[tt % 2]
            pairs = []
            for j in range(4):
                blocks = [(4 * tt + i, (0 if i == 3 else 128 * i), i) for i in (3, 2, 1, 0)]
                blocks += [(kb, 0, -1) for kb in range(4 * tt - 1, -1, -1)]
                for n, (kb, c0, di) in enumerate(blocks):
                    pairs.append([dict(h=2 * j + q, q=q, kb=kb, c0=c0, di=di, first=(n == 0), last=(n == len(blocks) - 1), i=2 * len(pairs) + q) for q in range(2)])
            npairs = len(pairs)

            def qk(s_):
                h, kb, c0, i = s_["h"], s_["kb"], s_["c0"], s_["i"]
                j, r0 = h // 2, (h % 2) * 64
                bk = i % NPB
                P.op("tensor", lambda e: e.matmul(PB[bk][:, c0:T], KT[r0:r0 + 64, j, kb * 128:(kb + 1) * 128], QT[r0:r0 + 64, j, c0:T], start=True, stop=False, skip_group_check=True),
                     reads=[b_KT[j][kb // 4], b_QT[j]], writes=[b_PB[bk]])

            def mask(s_):
                c0, i = s_["c0"], s_["i"]
                bk = i % NPB
                if s_["di"] == 3:
                    P.op("tensor", lambda e: e.matmul(PB[bk][:, 0:T], IDB, NEGF3, start=False, stop=False, skip_group_check=True), reads=[b_cst], writes=[b_PB[bk]])
                elif s_["di"] >= 0:
                    P.op("tensor", lambda e: e.matmul(PB[bk][:, c0:c0 + 128], IDB, NEGTRI, start=False, stop=False, skip_group_check=True), reads=[b_cst], writes=[b_PB[bk]])

            def e_op(pa_):
                c0, i = pa_[0]["c0"], pa_[0]["i"]
                pp = (i // 2) % 2
                P.op("scalar", lambda e: e.activation(out=e2[:, :, c0:T], in_=PBt[:, 2 * pp:2 * pp + 2, c0:T], func=AF.Exp), reads=[b_PB[2 * pp], b_PB[2 * pp + 1]], writes=[b_e[0]])

            def l_op(pa_):
                c0, i = pa_[0]["c0"], pa_[0]["i"]
                pp = (i // 2) % 2
                P.op("scalar", lambda e: e.activation(out=sp2[pp][:, :, c0:T], in_=e2[:, :, c0:T], func=AF.Ln, bias=1.0), reads=[b_e[0]], writes=[b_sp[pp]])

            def tri(s_):
                c0, i, q = s_["c0"], s_["i"], s_["q"]
                bk, pp = i % NPB, (i // 2) % 2
                P.op("tensor", lambda e: e.matmul(PB[bk][:, c0:T], NTRI, sp2[pp][:, q, c0:T], start=False, stop=s_["first"], skip_group_check=True),
                     reads=[b_sp[pp], b_cst], writes=[b_PB[bk]])
                if not s_["first"]:
                    P.op("tensor", lambda e: e.matmul(PB[bk][:, c0:T], nonesb[:], Srun[q][:, c0:T], start=False, stop=True, skip_group_check=True),
                         reads=[b_S[q], b_cst], writes=[b_PB[bk]])

            def supd(s_):
                c0, i, q = s_["c0"], s_["i"], s_["q"]
                pp = (i // 2) % 2
                if not s_["last"]:
                    if s_["first"]:
                        P.op("vector", lambda e: e.tensor_copy(out=Srun[q][:], in_=sp2[pp][:, q, :]), reads=[b_sp[pp]], writes=[b_S[q]])
                    else:
                        P.op("vector", lambda e: e.tensor_tensor(out=Srun[q][:, c0:T], in0=Srun[q][:, c0:T], in1=sp2[pp][:, q, c0:T], op=ALU.add), reads=[b_sp[pp], b_S[q]], writes=[b_S[q]])

            def xw(pa_):
                c0, i = pa_[0]["c0"], pa_[0]["i"]
                pp = (i // 2) % 2
                P.op("scalar", lambda e: e.activation(out=w2[pp][:, :, c0:T], in_=PBt[:, 2 * pp:2 * pp + 2, c0:T], func=AF.Exp), reads=[b_PB[2 * pp], b_PB[2 * pp + 1]], writes=[b_w[pp]])

            def wv(s_):
                h, kb, c0, i, q = s_["h"], s_["kb"], s_["c0"], s_["i"], s_["q"]
                j, r0 = h // 2, (h % 2) * 64
                pp = (i // 2) % 2
                po = j % NPO
                P.op("tensor", lambda e: e.matmul(PO[po][r0:r0 + 64, c0:T], Vt[:, kb, h * 64:(h + 1) * 64], w2[pp][:, q, c0:T], start=s_["first"], stop=s_["last"], skip_group_check=True),
                     reads=[b_V[kb // 4], b_w[pp]], writes=[b_PO[po]])
                if s_["last"] and h % 2 == 1:
                    evac_copy(yc[:, 4 + j, :], PO[po][:], [b_PO[po]], [b_yc[4 + j]], eng="vector")

            for it in range(npairs + 2):
                P.tile = tt
                if it < npairs:
                    pa_ = pairs[it]
                    qk(pa_[0]); qk(pa_[1])
                    mask(pa_[0]); mask(pa_[1])
                    e_op(pa_)
                if 0 <= it - 1 < npairs:
                    pa_ = pairs[it - 1]
                    tri(pa_[0]); tri(pa_[1])
                    supd(pa_[0]); supd(pa_[1])
                    xw(pa_)
                if it < npairs:
                    l_op(pairs[it])
                if 0 <= it - 2 < npairs:
                    pa_ = pairs[it - 2]
                    wv(pa_[0]); wv(pa_[1])
                yield

        def gen_C(tt):
            t0 = tt * T
            yc = YC[tt % 2]
            b_yc = b_YC[tt % 2]

            def st():
                P.tile = tt

            st()
            yield from load_x(tt)

            def ev_res(sl, m, bank):
                st()
                c = sl * 4 + m
                P.op("vector", lambda e: e.tensor_tensor(out=xT[:, c, :], in0=xT[:, c, :], in1=PG[bank][:], op=ALU.add), reads=[b_PG[bank], b_xT[c]], writes=[b_xT[c]])

            st()
            yield from proj_fm("w_out", 2, lambda kc: yc[:, kc, :], b_yc, ev_res)

            def mlp(grow):
                st()
                rms_norm()
                norm_apply_h(grow)
                yield
                for half in range(2):
                    def ev_w1(sl, m, bank):
                        st()
                        k = sl * 4 + m
                        q = k % 2
                        if k % 2 == 0:
                            P.op("scalar", lambda e: e.activation(out=ce_b[q][:], in_=PG[bank][:], func=AF.Square), reads=[b_PG[bank]], writes=[b_ce[q]])
                            P.op("vector", lambda e: e.scalar_tensor_tensor(out=aT_chunk(k), in0=PG[bank][:], scalar=0.0, in1=ce_b[q][:], op0=ALU.is_gt, op1=ALU.mult),
                                 reads=[b_PG[bank], b_ce[q]], writes=[b_RA[k // 2]])
                        else:
                            P.op("vector", lambda e: e.tensor_scalar(out=ce_b[q][:], in0=PG[bank][:], scalar1=0.0, scalar2=None, op0=ALU.max), reads=[b_PG[bank]], writes=[b_ce[q]])
                            P.op("vector", lambda e: e.tensor_tensor(out=aT_chunk(k), in0=ce_b[q][:], in1=ce_b[q][:], op=ALU.mult), reads=[b_ce[q]], writes=[b_RA[k // 2]])
                    st()
                    yield from proj_fm("w1", 4, lambda kc: hT[:, kc, :], b_hT, ev_w1)
                    for q4 in range(4):
                        slot = next_slab("w2")
                        wv_ = ring[slot][:].rearrange("p a b -> p (a b)").rearrange("p (kc c) -> p kc c", c=256)
                        banks = [pgbank(), pgbank()]
                        for mm in range(2):
                            for kq in range(2):
                                st()
                                for kc in range(kq * 8, kq * 8 + 8):
                                    P.op("tensor", lambda e, wv_=wv_, mm=mm, kc=kc, bank=banks[mm]: e.matmul(PG[bank][:], wv_[:, kc, mm * 128:(mm + 1) * 128], aT_chunk(kc), start=(kc == 0), stop=(kc == 15)),
                                         reads=[b_ring[slot], b_RA[kc // 2]], writes=[b_PG[banks[mm]]])
                                yield
                        for mm in range(2):
                            st()
                            c = q4 * 2 + mm
                            P.op("vector", lambda e, c=c, bank=banks[mm]: e.tensor_tensor(out=xT[:, c, :], in0=xT[:, c, :], in1=PG[bank][:], op=ALU.add), reads=[b_PG[banks[mm]], b_xT[c]], writes=[b_xT[c]])
                        release_slab()

            yield from mlp(R_MG0)

            st()
            rms_norm()
            norm_apply_h(R_ODG)
            P.op("vector", lambda e: e.tensor_copy(out=YU[:, :, 0:30], in_=UH[:]), reads=[b_UH], writes=b_YU)
            yield

            def ev_pw1(sl, m, bank):
                st()
                c = (sl % 2) * 4 + m
                if sl < 2:
                    P.op("vector", lambda e: e.tensor_scalar(out=aT_chunk(c), in0=PG[bank][:], scalar1=PAT[:, c, R_PW1B0:R_PW1B0 + 1], scalar2=None, op0=ALU.add),
                         reads=[b_PG[bank], b_par], writes=[b_RA[c // 2]])
                else:
                    q = c % 2
                    P.op("scalar", lambda e: e.activation(out=ce_b[q][:], in_=PG[bank][:], func=AF.Exp, scale=-1.0, bias=DER[:, 16 + c:17 + c]), reads=[b_PG[bank], b_par], writes=[b_ce[q]])
                    P.op("scalar", lambda e: e.activation(out=ce_b[q][:], in_=ce_b[q][:], func=AF.Ln, bias=1.0), reads=[b_ce[q]], writes=[b_ce[q]])
                    P.op("scalar", lambda e: e.activation(out=ce_b[q][:], in_=ce_b[q][:], func=AF.Exp, scale=-1.0), reads=[b_ce[q]], writes=[b_ce[q]])
                    P.op("vector", lambda e: e.tensor_tensor(out=YU[:, c, 30:T + 30], in0=aT_chunk(c), in1=ce_b[q][:], op=ALU.mult), reads=[b_ce[q], b_RA[c // 2]], writes=[b_YU[c]])

            yield from proj_fm("pw1", 4, lambda kc: hT[:, kc, :], b_hT, ev_pw1)
            st()
            P.op("vector", lambda e: e.tensor_copy(out=UH[:], in_=YU[:, :, T:T + 30]), reads=b_YU, writes=[b_UH])

            dgi = {"n": 0}
            for c in range(8):
                bank = pgbank()
                for k in range(31):
                    st()
                    di = dgi["n"] % NDG
                    dgi["n"] += 1
                    P.op("vector", lambda e, c=c, k=k, di=di: e.tensor_scalar(out=dg31[:, di, :], in0=IDB, scalar1=PAT[:, c, R_DWW + k:R_DWW + k + 1], scalar2=None, op0=ALU.mult),
                         reads=[b_par, b_cst], writes=[b_dg[di]])
                    P.op("tensor", lambda e, c=c, k=k, di=di, bank=bank: e.matmul(PG[bank][:], dg31[:, di, :], YU[:, c, k:k + T], start=(k == 0), stop=(k == 30)),
                         reads=[b_dg[di], b_YU[c]], writes=[b_PG[bank]])
                    if k % 8 == 7:
                        yield
                st()
                P.op("scalar", lambda e, c=c, bank=bank: e.activation(out=RA[:, c, :], in_=PG[bank][:], func=AF.Identity, bias=PAT[:, c, R_DWB:R_DWB + 1]), reads=[b_PG[bank], b_par], writes=[b_RA[c]])
                yield

            st()
            bsum = pgbank()
            for c in range(8):
                q = c % 2
                P.op("vector", lambda e, c=c, q=q: e.tensor_copy(out=sq_b[q][:], in_=RA[:, c, :]), reads=[b_RA[c]], writes=[b_sq[q]])
                P.op("tensor", lambda e, c=c, q=q, bsum=bsum: e.matmul(PG[bsum][:], onesb[:], sq_b[q][:], start=(c == 0), stop=(c == 7)), reads=[b_sq[q], b_cst], writes=[b_PG[bsum]])
            yield
            st()
            bsq = pgbank()
            for c in range(8):
                q = c % 2
                P.op("scalar", lambda e, c=c, q=q: e.activation(out=cw_b[q][:], in_=RA[:, c, :], func=AF.Square), reads=[b_RA[c]], writes=[b_cw[q]])
                P.op("tensor", lambda e, c=c, q=q, bsq=bsq: e.matmul(PG[bsq][:], onesb[:], cw_b[q][:], start=(c == 0), stop=(c == 7)), reads=[b_cw[q], b_cst], writes=[b_PG[bsq]])
            yield
            st()
            P.op("vector", lambda e, bsum=bsum: e.tensor_scalar(out=lnv[:], in0=PG[bsum][:], scalar1=1.0 / D, scalar2=None, op0=ALU.mult), reads=[b_PG[bsum]], writes=[b_lnv])
            P.op("vector", lambda e: e.tensor_tensor(out=ce_b[0][:], in0=lnv[:], in1=lnv[:], op=ALU.mult), reads=[b_lnv], writes=[b_ce[0]])
            P.op("vector", lambda e, bsq=bsq: e.scalar_tensor_tensor(out=ce_b[0][:], in0=PG[bsq][:], scalar=1.0 / D, in1=ce_b[0][:], op0=ALU.mult, op1=ALU.subtract), reads=[b_PG[bsq], b_ce[0]], writes=[b_ce[0]])
            P.op("scalar", lambda e: e.activation(out=ce_b[0][:], in_=ce_b[0][:], func=AF.Ln, bias=1e-5), reads=[b_ce[0]], writes=[b_ce[0]])
            P.op("scalar", lambda e: e.activation(out=rstd[:], in_=ce_b[0][:], func=AF.Exp, scale=-0.5), reads=[b_ce[0]], writes=[b_rstd])
            for c in range(8):
                st()
                q = c % 2
                P.op("vector", lambda e, c=c: e.tensor_tensor(out=RA[:, c, :], in0=RA[:, c, :], in1=lnv[:], op=ALU.subtract), reads=[b_RA[c], b_lnv], writes=[b_RA[c]])
                P.op("vector", lambda e, c=c: e.tensor_tensor(out=RA[:, c, :], in0=RA[:, c, :], in1=rstd[:], op=ALU.mult), reads=[b_RA[c], b_rstd], writes=[b_RA[c]])
                P.op("scalar", lambda e, c=c, q=q: e.activation(out=ce_b[q][:], in_=RA[:, c, :], func=AF.Exp, scale=DER[:, 24 + c:25 + c], bias=DER[:, 32 + c:33 + c]), reads=[b_RA[c], b_par], writes=[b_ce[q]])
                P.op("scalar", lambda e, q=q: e.activation(out=ce_b[q][:], in_=ce_b[q][:], func=AF.Ln, bias=1.0), reads=[b_ce[q]], writes=[b_ce[q]])
                P.op("scalar", lambda e, q=q: e.activation(out=ce_b[q][:], in_=ce_b[q][:], func=AF.Exp, scale=-1.0), reads=[b_ce[q]], writes=[b_ce[q]])
                P.op("vector", lambda e, c=c: e.tensor_scalar(out=RA[:, c, :], in0=RA[:, c, :], scalar1=PAT[:, c, R_LNG:R_LNG + 1], scalar2=PAT[:, c, R_LNB:R_LNB + 1], op0=ALU.mult, op1=ALU.add),
                     reads=[b_RA[c], b_par], writes=[b_RA[c]])
                P.op("vector", lambda e, c=c, q=q: e.tensor_tensor(out=hT[:, c, :], in0=RA[:, c, :], in1=ce_b[q][:], op=ALU.mult), reads=[b_RA[c], b_ce[q]], writes=[b_hT[c]])
                yield

            def ev_pw2(sl, m, bank):
                st()
                c = sl * 4 + m
                P.op("vector", lambda e: e.scalar_tensor_tensor(out=xT[:, c, :], in0=PG[bank][:], scalar=PAT[:, c, R_PW2B:R_PW2B + 1], in1=xT[:, c, :], op0=ALU.add, op1=ALU.add),
                     reads=[b_PG[bank], b_xT[c], b_par], writes=[b_xT[c]])

            yield from proj_fm("pw2", 2, lambda kc: hT[:, kc, :], b_hT, ev_pw2)

            yield from mlp(R_MG1)

            st()
            rms_norm()
            for c in range(8):
                P.op("vector", lambda e, c=c: e.scalar_tensor_tensor(out=RA[:, c, :], in0=xT[:, c, :], scalar=PAT[:, c, R_FG:R_FG + 1], in1=rstd[:], op0=ALU.mult, op1=ALU.mult),
                     reads=[b_xT[c], b_rstd, b_par], writes=[b_RA[c]])
            yield
            for s in range(4):
                st()
                oi = s % 2
                for half in range(2):
                    bank = pgbank()
                    for cc in range(4):
                        c = half * 4 + cc
                        P.op("tensor", lambda e, c=c, cc=cc, s=s, bank=bank: e.transpose(out=PG[bank][:, cc * 128:(cc + 1) * 128], in_=RA[:, c, s * 128:(s + 1) * 128], identity=IDF),
                             reads=[b_RA[c], b_cst], writes=[b_PG[bank]])
                    evac_copy(ost[oi][:, half * 512:(half + 1) * 512], PG[bank][:], [b_PG[bank]], [b_ost[oi]])
                dma("gpsimd", y_d[t0 + s * 128:t0 + (s + 1) * 128, :], ost[oi][:], ost_sem[oi], reads=[b_ost[oi]])
                yield

        NC_EST = 214.0

        def interleave(gb, gc, nb):
            import os
            if os.environ.get("KSEQ") == "1":
                run(gb)
                run(gc)
                return
            if os.environ.get("KSEQ") == "2":
                run(gc)
                run(gb)
                return
            ratio = NC_EST / max(nb, 1)
            acc = 0.0
            c_done = False
            for _ in range(nb):
                next(gb, None)
                acc += ratio
                while acc >= 1.0 and not c_done:
                    acc -= 1.0
                    try:
                        next(gc)
                    except StopIteration:
                        c_done = True
            run(gb)
            if not c_done:
                run(gc)

        stage_A(0)
        run(gen_B(0))
        for tt in range(NT):
            if tt + 1 < NT:
                stage_A(tt + 1)
                nb = 4 * (4 * (tt + 1) + 4) + 2
                interleave(gen_B(tt + 1), gen_C(tt), nb)
            else:
                run(gen_C(tt))

        P.tile = NT
        last = [o for o in P.ops["gpsimd"] if o.dma and o.sem in ost_sem]
        P.op("gpsimd", lambda e: e.nop(), extra=last[-2:])

        P.finalize(engsems)
        blk = es.enter_context(nc.Block())

        @blk.sync
        def _(e):
            P.emit("sync", e)

        @blk.scalar
        def _(e):
            P.emit("scalar", e)

        @blk.gpsimd
        def _(e):
            P.emit("gpsimd", e)

        @blk.vector
        def _(e):
            P.emit("vector", e)

        @blk.tensor
        def _(e):
            P.emit("tensor", e)

    return nc


def make_consts():
    c = np.zeros((128, 896), np.float32)
    c[:, 0:128] = np.eye(128, dtype=np.float32)
    j = np.arange(128)[:, None]
    s = np.arange(128)[None, :]
    c[:, 128:256] = np.where(j >= s, -1.0, 0.0)
    c[:, 256:384] = np.where(j >= s, NEG, 0.0)
    c[:, 384:768] = NEG
    c[:, 768:896] = c[:, 256:384]
    return c


def pack_inputs(inp, NT=8):
    f = lambda a: np.ascontiguousarray(np.asarray(a, dtype=np.float32))
    pa = np.zeros((NPA, D), np.float32)
    pa[R_EVG] = inp["ev_norm_g"][0]
    pa[R_ODG] = inp["od_norm_g"][0]
    pa[R_MG0] = inp["mlp_norm_g"][0]
    pa[R_MG1] = inp["mlp_norm_g"][1]
    pa[R_FG] = inp["final_g"]
    pa[R_DWB] = inp["od_dw_b"][0]
    pa[R_LNG] = inp["od_ln_g"][0]
    pa[R_LNB] = inp["od_ln_b"][0]
    pa[R_PW2B] = inp["od_pw2_b"][0]
    pa[R_PW1B0] = inp["od_pw1_b"][0][:D]
    pa[R_PW1B1] = inp["od_pw1_b"][0][D:]
    pa[R_DWW:R_DWW + 31] = inp["od_dw_w"][0]
    pb = np.zeros((8, 512), np.float32)
    pb[0:4] = inp["ev_conv_w"][0]
    pb[4] = inp["ev_conv_b"][0]
    pb[5] = inp["ev_gate_a_b"][0]
    pb[6] = inp["ev_gate_x_b"][0]
    pb[7] = inp["ev_lam"][0]
    shared = {
        "w_in": f(inp["ev_w_in"][0]), "w_out": f(inp["ev_w_out"][0]),
        "w1": f(inp["mlp_w1"]), "w2": f(inp["mlp_w2"]),
        "pw1": f(inp["od_pw1_w"][0]), "pw2": f(inp["od_pw2_w"][0]),
        "pa": pa, "pb": pb, "gaw": f(inp["ev_gate_a_w"][0]), "gxw": f(inp["ev_gate_x_w"][0]),
        "cst": make_consts(),
    }
    x = f(inp["x"])
    return [dict(shared, x=x[b]) for b in range(x.shape[0])]


_NC_CACHE = {}


def kernel(**inputs):
    inputs = {k: np.asarray(v) for k, v in inputs.items()}
    if "nc" not in _NC_CACHE:
        _NC_CACHE["nc"] = build(8)
    nc = _NC_CACHE["nc"]
    in_maps = pack_inputs(inputs)
    res = run_bass_kernel_spmd(nc, in_maps, core_ids=list(range(8)))
    return np.stack([np.asarray(r["y"], dtype=np.float32) for r in res.results], axis=0)
```

```python
from contextlib import ExitStack

import numpy as np
import concourse.bass as bass
import concourse.mybir as mybir
from concourse.bass_utils import run_bass_kernel_spmd

F32 = mybir.dt.float32
BF16 = mybir.dt.bfloat16
AF = mybir.ActivationFunctionType
ALU = mybir.AluOpType

D = 1024
S = 4096
T = 512
NSLOT = 3
NEG = -30000.0
ENGS = ("tensor", "vector", "scalar", "gpsimd", "sync")

R_EVG, R_ODG, R_MG0, R_MG1, R_FG, R_DWB, R_LNG, R_LNB, R_PW2B, R_PW1B0, R_PW1B1, R_DWW = 0, 1, 2, 3, 4, 5, 6, 7, 8, 9, 10, 11
NPA = 42


class Sem:
    def __init__(self, h):
        self.h = h
        self.count = 0


class Op:
    __slots__ = ("eng", "fn", "deps", "sem", "val", "dma", "used", "tile")

    def __init__(self, eng, fn, deps, dma_sem, tile):
        self.eng = eng
        self.fn = fn
        self.deps = deps
        self.sem = dma_sem
        self.val = None
        self.dma = dma_sem is not None
        self.used = False
        self.tile = tile


class Buf:
    __slots__ = ("name", "w", "r")

    def __init__(self, name):
        self.name = name
        self.w = None
        self.r = []


class Prog:
    def __init__(self):
        self.ops = {e: [] for e in ENGS}
        self.tile = 0

    def op(self, eng, fn, reads=(), writes=(), dma_sem=None, extra=(), nochain=False):
        deps = set(extra)
        for b in reads:
            if b.w is not None:
                deps.add(b.w)
        for b in writes:
            if b.w is not None:
                deps.add(b.w)
            deps.update(b.r)
        o = Op(eng, fn, deps, dma_sem, self.tile)
        deps.discard(o)
        if dma_sem is not None:
            deps = {d for d in deps if not (d.dma and d.sem is dma_sem and d.eng == eng)} if nochain else deps
            o.deps = deps
        for b in reads:
            b.r.append(o)
        for b in writes:
            b.w = o
            b.r = []
        self.ops[eng].append(o)
        return o

    def finalize(self, engsems):
        for e in ENGS:
            for o in self.ops[e]:
                for d in o.deps:
                    if d.eng == "tensor" and o.eng == "tensor":
                        continue
                    d.used = True
        for e in ENGS:
            for o in self.ops[e]:
                if o.dma:
                    o.sem.count += 16
                    o.val = o.sem.count
                elif o.used:
                    s = engsems[e][o.tile]
                    s.count += 1
                    o.sem = s
                    o.val = s.count

    def emit(self, eng_name, eng):
        waited = {}
        for o in self.ops[eng_name]:
            need = {}
            for d in o.deps:
                if d.eng == "tensor" and o.eng == "tensor":
                    continue
                if d.sem is None:
                    continue
                if need.get(d.sem, 0) < d.val:
                    need[d.sem] = d.val
            for s, v in need.items():
                if waited.get(s, 0) < v:
                    eng.wait_ge(s.h, v)
                    waited[s] = v
            ins = o.fn(eng)
            if o.sem is not None and (o.dma or o.used):
                ins.then_inc(o.sem.h, 16 if o.dma else 1)


def slab_table():
    t = []
    for nb in range(5):
        t.append(("w_in", 0, 0, nb))
    for nb in range(2):
        t.append(("w_out", 0, 0, nb))

    def mlp(l):
        for half in range(2):
            for nb in range(4):
                t.append(("w1", l, 0, half * 4 + nb))
            for q4 in range(4):
                t.append(("w2", l, half, q4))

    mlp(0)
    for nb in range(4):
        t.append(("pw1", 0, 0, nb))
    for nb in range(2):
        t.append(("pw2", 0, 0, nb))
    mlp(1)
    return t


SLABS = slab_table()
NSLAB = len(SLABS)


def build(NT=8, dbg=None):
    nc = bass.Bass("TRN2", target_bir_lowering=False)
    P = Prog()
    ntok = NT * T

    def din(name, shape):
        return nc.dram_tensor(name, shape, F32, kind="ExternalInput").ap()

    x_d = din("x", [S, D])
    w_d = {
        "w_in": din("w_in", [D, 2560]),
        "w_out": din("w_out", [D, D]),
        "w1": din("w1", [2, D, 4 * D]),
        "w2": din("w2", [2, 4 * D, D]),
        "pw1": din("pw1", [D, 2 * D]),
        "pw2": din("pw2", [D, D]),
    }
    pa_d = din("pa", [NPA, D])
    pb_d = din("pb", [8, 512])
    gaw_d = din("gaw", [8, 64, 64])
    gxw_d = din("gxw", [8, 64, 64])
    cst_d = din("cst", [128, 896])
    y_d = nc.dram_tensor("y", [S, D], F32, kind="ExternalOutput").ap()
    wscr = nc.dram_tensor("wscr", [NSLAB, 128, 4096], BF16, kind="Internal").ap()
    dbg_out = {}

    es = ExitStack()
    with es:
        def sb(name, shape, dt=F32):
            return es.enter_context(nc.sbuf_tensor(name, shape, dt))

        def newsem(name):
            return Sem(es.enter_context(nc.semaphore(name)))

        KT = sb("KT", [128, 4, S], BF16)
        Vt = sb("Vt", [128, 32, 512], BF16)
        xT = sb("xT", [128, 8, T])
        io_b = [sb(f"io{i}", [128, D]) for i in range(2)]
        xin = io_b
        ost = io_b
        hT = sb("hT", [128, 8, T], BF16)
        RA = sb("RA", [128, 8, T])
        XRb = sb("XRb", [128, 4, T + 3], BF16)
        QT = sb("QT", [128, 4, T], BF16)
        YC = [sb(f"YC{i}", [128, 8, T], BF16) for i in range(2)]
        YU = sb("YU", [128, 8, T + 30], BF16)
        ce_b = [sb(f"ce{i}", [128, T]) for i in range(2)]
        cw_b = [sb(f"cw{i}", [128, T], BF16) for i in range(2)]
        UH = sb("UH", [128, 8, 30], BF16)
        e2 = sb("e2", [128, 2, T])
        e_b = [e2[:, i, :] for i in range(2)]
        sp2 = [sb(f"sp2_{i}", [128, 2, T], BF16) for i in range(2)]
        w2 = [sb(f"w2_{i}", [128, 2, T], BF16) for i in range(2)]
        Srun = [sb(f"Srun{i}", [128, T], BF16) for i in range(2)]
        sq_b = [sb(f"sq{i}", [128, T], BF16) for i in range(2)]
        rstd = sb("rstd", [128, T])
        lnv = sb("lnv", [128, T])
        xcb = sq_b[0]
        ring = [sb(f"ring{i}", [128, 8, 512], BF16) for i in range(NSLOT)]
        cstf = sb("cstf", [128, 128])
        cstb = sb("cstb", [128, 896], BF16)
        onesb = sb("onesb", [128, 128], BF16)
        nonesb = sb("nonesb", [128, 128], BF16)
        PAT = sb("PAT", [128, 8, NPA])
        PBT = sb("PBT", [128, 4, 8])
        DER = sb("DER", [128, 48])
        BDb = sb("BDb", [128, 4, 2, 128], BF16)
        DG4 = sb("DG4", [128, 4, 4, 128], BF16)
        NDG = 6
        dg31 = sb("dg31", [128, NDG, 128], BF16)
        hcar = sb("hcar", [128, 4])

        pas = RA[0:NPA, 0:2, :].rearrange("p a b -> p (a b)")
        pbs = RA[0:8, 2, :]
        BDf = RA[:, 3:5, :].rearrange("p a b -> p (a b)").rearrange("p (a b c) -> p a b c", a=4, b=2)
        xstA = [hT[:, 4 * i:4 * i + 4, :].rearrange("p a b -> p (a b)").bitcast(F32) for i in range(2)]
        IDF = cstf[:, 0:128]
        IDB = cstb[:, 0:128]
        NTRI = cstb[:, 128:256]
        NEGTRI = cstb[:, 256:384]
        NEGF3 = cstb[:, 384:896]


        def aT_chunk(k):
            v = RA[:, k // 2, :].bitcast(BF16)
            return v[:, (k % 2) * T:(k % 2 + 1) * T]

        NPB = 4
        NPG = 3
        NPO = 1
        PBt = es.enter_context(nc.psum_tensor("pbt", [128, NPB, T], F32))
        PB = [PBt[:, i, :] for i in range(NPB)]
        PO = [es.enter_context(nc.psum_tensor(f"po{i}", [128, T], F32)) for i in range(NPO)]
        PG = [es.enter_context(nc.psum_tensor(f"pg{i}", [128, T], F32)) for i in range(NPG)]

        engsems = {e: [newsem(f"s_{e}_{t}") for t in range(NT + 1)] for e in ("tensor", "vector", "scalar", "gpsimd")}
        engsems["sync"] = [None] * (NT + 1)
        ring_sem = [newsem(f"ring{i}") for i in range(NSLOT)]
        cast_sem = [newsem(f"cast{i}") for i in range(4)]
        xin_sem = [newsem(f"xin{i}") for i in range(2)]
        xa_sem = [newsem(f"xa{i}") for i in range(2)]
        ost_sem = [newsem(f"ost{i}") for i in range(2)]
        setup_sem = newsem("setup")
        setup_sem2 = newsem("setup2")

        def B(name):
            return Buf(name)

        b_KT = [[B(f"KT{j}_{t}") for t in range(NT)] for j in range(4)]
        b_V = [B(f"V{t}") for t in range(NT)]
        b_xT = [B(f"xT{c}") for c in range(8)]
        b_xin = [B("io0"), B("io1")]
        b_ost = b_xin
        b_hT = [B(f"hT{c}") for c in range(8)]
        b_RA = [B(f"RA{c}") for c in range(8)]
        b_XR = [B(f"XR{m}") for m in range(4)]
        b_XRh = B("XRh")
        b_QT = [B(f"QT{m}") for m in range(4)]
        b_YU = [B(f"YU{c}") for c in range(8)]
        b_YC = [[B(f"YC{i}_{c}") for c in range(8)] for i in range(2)]
        b_ce = [B("ce0"), B("ce1")]
        b_cw = [B("cw0"), B("cw1")]
        b_UH = B("UH")
        b_e = [B("e2")]
        b_sp = [B(f"sp2_{i}") for i in range(2)]
        b_w = [B(f"w2_{i}") for i in range(2)]
        b_S = [B("S0"), B("S1")]
        b_sq = [B("sq0"), B("sq1")]
        b_rstd = B("rstd")
        b_lnv = B("lnv")
        b_xcb = b_sq[0]
        b_ring = [B(f"ring{i}") for i in range(NSLOT)]
        b_PB = [B(f"PB{i}") for i in range(NPB)]
        b_PO = [B(f"PO{i}") for i in range(NPO)]
        b_PG = [B(f"PG{i}") for i in range(NPG)]
        b_wscr = [B(f"wscr{i}") for i in range(NSLAB)]
        b_cst = B("cst")
        b_par = B("params")
        b_dg = [B(f"dg{i}") for i in range(NDG)]
        b_hcar = B("hcar")
        b_stg = b_RA[0:5]

        def dma(q, out, in_, sem, reads=(), writes=(), nochain=False):
            return P.op(q, lambda e, out=out, in_=in_: e.dma_start(out=out, in_=in_), reads=reads, writes=writes, dma_sem=sem, nochain=nochain)

        dma("gpsimd", cstb[:], cst_d[:, :], setup_sem2, writes=[b_cst])
        cast_ops = []
        for i, (key, l, ks, nb) in enumerate(SLABS):
            W = w_d[key]
            W2 = W[l] if key in ("w1", "w2") else W
            if key == "w2":
                src = W2[ks * 2048:(ks + 1) * 2048, nb * 256:(nb + 1) * 256].rearrange("(kc p) c -> p kc c", p=128)
                dst = wscr[i].rearrange("p (kc c) -> p kc c", c=256)
            else:
                src = W2[ks * 1024:(ks + 1) * 1024, nb * 512:(nb + 1) * 512].rearrange("(kc p) c -> p kc c", p=128)
                dst = wscr[i].rearrange("p (kc c) -> p kc c", c=512)
            cast_ops.append(P.op("gpsimd", lambda e, dst=dst, src=src: e.dma_start(out=dst, in_=src), writes=[b_wscr[i]], dma_sem=cast_sem[i % 4],
                                 extra=(cast_ops[i - 3:i - 2] if i >= 3 else ())))

        dma("sync", cstf[:], cst_d[:, 0:128], setup_sem, writes=[b_cst], nochain=True)
        dma("sync", pas[:], pa_d[:, :], setup_sem, writes=[b_par] + b_stg, nochain=True)
        dma("sync", pbs[:], pb_d[:, :], setup_sem, writes=[b_par] + b_stg, nochain=True)
        P.op("vector", lambda e: e.memset(BDf.rearrange("p a b c -> p (a b c)"), 0.0), writes=[b_par] + b_stg)
        for h in range(8):
            r0 = (h % 2) * 64
            dma("sync", BDf[r0:r0 + 64, h // 2, 0, r0:r0 + 64], gaw_d[h], setup_sem, writes=[b_par] + b_stg, nochain=True)
            dma("sync", BDf[r0:r0 + 64, h // 2, 1, r0:r0 + 64], gxw_d[h], setup_sem, writes=[b_par] + b_stg, nochain=True)

        P.op("vector", lambda e: e.memset(onesb[:], 1.0), writes=[b_cst])
        P.op("vector", lambda e: e.memset(nonesb[:], -1.0), writes=[b_cst])
        P.op("vector", lambda e: e.memset(hcar[:], 0.0), writes=[b_hcar])
        P.op("vector", lambda e: e.memset(XRb[:, :, 0:3], 0.0), writes=[b_XRh])
        P.op("vector", lambda e: e.memset(UH[:].rearrange("p a b -> p (a b)"), 0.0), writes=[b_UH])
        P.op("vector", lambda e: e.tensor_copy(out=BDb[:].rearrange("p a b c -> p (a b c)"), in_=BDf.rearrange("p a b c -> p (a b c)")), reads=[b_par] + b_stg, writes=[b_par])

        for c in range(8):
            P.op("tensor", lambda e, c=c: e.transpose(out=PG[0][:, c * NPA:(c + 1) * NPA], in_=pas[:, c * 128:(c + 1) * 128], identity=IDF[0:NPA, 0:NPA]),
                 reads=[b_par, b_cst] + b_stg, writes=[b_PG[0]])
        P.op("vector", lambda e: e.tensor_copy(out=PAT[:].rearrange("p c r -> p (c r)"), in_=PG[0][:, 0:8 * NPA]), reads=[b_PG[0]], writes=[b_par])
        for c in range(4):
            P.op("tensor", lambda e, c=c: e.transpose(out=PG[1][:, c * 8:(c + 1) * 8], in_=pbs[:, c * 128:(c + 1) * 128], identity=IDF[0:8, 0:8]),
                 reads=[b_par, b_cst] + b_stg, writes=[b_PG[1]])
        P.op("vector", lambda e: e.tensor_copy(out=PBT[:].rearrange("p c r -> p (c r)"), in_=PG[1][:, 0:32]), reads=[b_PG[1]], writes=[b_par])

        P.op("vector", lambda e: e.tensor_scalar(out=DER[:, 0:4], in0=PBT[:, :, 5], scalar1=-1.0, scalar2=None, op0=ALU.mult), reads=[b_par], writes=[b_par])
        P.op("vector", lambda e: e.tensor_scalar(out=DER[:, 4:8], in0=PBT[:, :, 6], scalar1=-1.0, scalar2=None, op0=ALU.mult), reads=[b_par], writes=[b_par])
        P.op("scalar", lambda e: e.activation(out=DER[:, 40:44], in_=PBT[:, :, 7], func=AF.Exp, scale=-1.0), reads=[b_par], writes=[b_par])
        P.op("scalar", lambda e: e.activation(out=DER[:, 44:48], in_=DER[:, 40:44], func=AF.Ln, bias=1.0), reads=[b_par], writes=[b_par])
        P.op("vector", lambda e: e.tensor_scalar(out=DER[:, 8:12], in0=DER[:, 44:48], scalar1=-8.0, scalar2=None, op0=ALU.mult), reads=[b_par], writes=[b_par])
        P.op("vector", lambda e: e.tensor_scalar(out=DER[:, 12:16], in0=DER[:, 44:48], scalar1=-16.0, scalar2=None, op0=ALU.mult), reads=[b_par], writes=[b_par])
        P.op("vector", lambda e: e.tensor_scalar(out=DER[:, 16:24], in0=PAT[:, :, R_PW1B1], scalar1=-1.0, scalar2=None, op0=ALU.mult), reads=[b_par], writes=[b_par])
        P.op("vector", lambda e: e.tensor_scalar(out=DER[:, 24:32], in0=PAT[:, :, R_LNG], scalar1=-1.0, scalar2=None, op0=ALU.mult), reads=[b_par], writes=[b_par])
        P.op("vector", lambda e: e.tensor_scalar(out=DER[:, 32:40], in0=PAT[:, :, R_LNB], scalar1=-1.0, scalar2=None, op0=ALU.mult), reads=[b_par], writes=[b_par])
        for m in range(4):
            for k in range(4):
                P.op("vector", lambda e, m=m, k=k: e.tensor_scalar(out=DG4[:, m, k, :], in0=IDB, scalar1=PBT[:, m, k:k + 1], scalar2=None, op0=ALU.mult),
                     reads=[b_par, b_cst], writes=[b_par])

        st = {"next_load": 0, "next_use": 0}
        seq = list(range(5))
        for t_ in range(NT):
            if t_ + 1 < NT:
                seq += list(range(5))
            seq += list(range(5, NSLAB))
        total_slabs = len(seq)

        def issue_load(gi):
            slot = gi % NSLOT
            i = seq[gi]
            dma("sync", ring[slot][:].rearrange("p a b -> p (a b)"), wscr[i], ring_sem[slot], reads=[b_wscr[i]], writes=[b_ring[slot]])

        for gi in range(min(NSLOT, total_slabs)):
            issue_load(gi)
        st["next_load"] = min(NSLOT, total_slabs)

        def next_slab(expect):
            gi = st["next_use"]
            assert SLABS[seq[gi]][0] == expect, (SLABS[seq[gi]], expect)
            st["next_use"] += 1
            return gi % NSLOT

        def release_slab():
            if st["next_load"] < total_slabs:
                issue_load(st["next_load"])
                st["next_load"] += 1

        rr = {"pg": 0, "ev": 0}

        def pgbank():
            rr["pg"] = (rr["pg"] + 1) % NPG
            return rr["pg"]

        def evac_copy(out_ap, in_ap, reads, writes, eng=None, scale=None):
            if eng is None:
                rr["ev"] ^= 1
                eng = "vector" if rr["ev"] else "scalar"
            if eng == "vector":
                if scale is None:
                    return P.op("vector", lambda e: e.tensor_copy(out=out_ap, in_=in_ap), reads=reads, writes=writes)
                return P.op("vector", lambda e: e.tensor_scalar(out=out_ap, in0=in_ap, scalar1=scale, scalar2=None, op0=ALU.mult), reads=reads, writes=writes)
            sc = 1.0 if scale is None else scale
            return P.op("scalar", lambda e: e.activation(out=out_ap, in_=in_ap, func=AF.Identity, scale=sc), reads=reads, writes=writes)

        def run(gen):
            for _ in gen:
                pass

        def load_x(tt, in_A=False):
            t0 = tt * T
            for s in range(4):
                bi = s % 2
                if in_A:
                    stg, bst, ssem = xstA[bi], b_hT[4 * bi:4 * bi + 4], xa_sem[bi]
                else:
                    stg, bst, ssem = xin[bi][:], [b_xin[bi]], xin_sem[bi]
                dma("sync", stg, x_d[t0 + s * 128:t0 + (s + 1) * 128, :], ssem, writes=bst)
                for half in range(2):
                    bank = pgbank()
                    for cc in range(4):
                        c = half * 4 + cc
                        P.op("tensor", lambda e, stg=stg, c=c, cc=cc, bank=bank: e.transpose(out=PG[bank][:, cc * 128:(cc + 1) * 128], in_=stg[:, c * 128:(c + 1) * 128], identity=IDF),
                             reads=list(bst) + [b_cst], writes=[b_PG[bank]])
                    P.op("vector", lambda e, half=half, s=s, bank=bank: e.tensor_copy(out=xT[:, half * 4:half * 4 + 4, s * 128:(s + 1) * 128], in_=PG[bank][:].rearrange("p (c t) -> p c t", t=128)),
                         reads=[b_PG[bank]], writes=[b_xT[half * 4 + i] for i in range(4)])
                yield

        def rms_norm():
            bank = pgbank()
            for c in range(8):
                q = c % 2
                P.op("scalar", lambda e, c=c, q=q: e.activation(out=sq_b[q][:], in_=xT[:, c, :], func=AF.Square), reads=[b_xT[c]], writes=[b_sq[q]])
                P.op("tensor", lambda e, c=c, q=q, bank=bank: e.matmul(PG[bank][:], onesb[:], sq_b[q][:], start=(c == 0), stop=(c == 7)),
                     reads=[b_sq[q], b_cst], writes=[b_PG[bank]])
            P.op("scalar", lambda e, bank=bank: e.activation(out=lnv[:], in_=PG[bank][:], func=AF.Ln, scale=1.0 / D, bias=1e-6), reads=[b_PG[bank]], writes=[b_lnv])
            P.op("scalar", lambda e: e.activation(out=rstd[:], in_=lnv[:], func=AF.Exp, scale=-0.5), reads=[b_lnv], writes=[b_rstd])

        def norm_apply_h(grow):
            for c in range(8):
                P.op("vector", lambda e, c=c: e.scalar_tensor_tensor(out=hT[:, c, :], in0=xT[:, c, :], scalar=PAT[:, c, grow:grow + 1], in1=rstd[:], op0=ALU.mult, op1=ALU.mult),
                     reads=[b_xT[c], b_rstd, b_par], writes=[b_hT[c]])

        def proj_fm(key, nslabs, src_chunks, b_src, evac):
            for sl in range(nslabs):
                slot = next_slab(key)
                for m in range(4):
                    bank = pgbank()
                    for kc in range(8):
                        P.op("tensor", lambda e, slot=slot, m=m, kc=kc, bank=bank: e.matmul(PG[bank][:], ring[slot][:, kc, m * 128:(m + 1) * 128], src_chunks(kc), start=(kc == 0), stop=(kc == 7)),
                             reads=[b_ring[slot], b_src[kc]], writes=[b_PG[bank]])
                    evac(sl, m, bank)
                    yield
                release_slab()

        def stage_A(tt):
            P.tile = tt
            t0 = tt * T
            yc = YC[tt % 2]
            b_yc = b_YC[tt % 2]
            run(load_x(tt, in_A=True))
            rms_norm()
            norm_apply_h(R_EVG)
            if tt > 0:
                P.op("vector", lambda e: e.tensor_copy(out=XRb[:, :, 0:3], in_=XRb[:, :, T:T + 3]), reads=b_XR, writes=[b_XRh])

            def ev_in(sl, m, bank):
                if sl == 0:
                    G = RA[:, m, :]
                    bG = b_RA[m]
                    Tg = RA[:, 4 + m, :]
                    bTg = b_RA[4 + m]
                    evac_copy(G, PG[bank][:], [b_PG[bank]], [bG], eng="vector")
                    P.op("scalar", lambda e: e.activation(out=Tg, in_=G, func=AF.Square), reads=[bG], writes=[bTg])
                    P.op("vector", lambda e: e.tensor_scalar(out=Tg, in0=Tg, scalar1=0.044715, scalar2=1.0, op0=ALU.mult, op1=ALU.add), reads=[bTg], writes=[bTg])
                    P.op("vector", lambda e: e.tensor_tensor(out=Tg, in0=Tg, in1=G, op=ALU.mult), reads=[bTg, bG], writes=[bTg])
                    P.op("scalar", lambda e: e.activation(out=Tg, in_=Tg, func=AF.Exp, scale=-1.5957691216057308), reads=[bTg], writes=[bTg])
                    P.op("scalar", lambda e: e.activation(out=Tg, in_=Tg, func=AF.Ln, bias=1.0), reads=[bTg], writes=[bTg])
                    P.op("scalar", lambda e: e.activation(out=Tg, in_=Tg, func=AF.Exp, scale=-1.0), reads=[bTg], writes=[bTg])
                    P.op("vector", lambda e: e.tensor_tensor(out=G, in0=Tg, in1=G, op=ALU.mult), reads=[bTg, bG], writes=[bG])
                elif sl == 1:
                    evac_copy(XRb[:, m, 3:T + 3], PG[bank][:], [b_PG[bank], b_XRh], [b_XR[m]])
                elif sl == 2:
                    evac_copy(QT[:, m, :], PG[bank][:], [b_PG[bank]], [b_QT[m]], eng="vector", scale=0.125)
                else:
                    evac_copy(KT[:, m, t0:t0 + T], PG[bank][:], [b_PG[bank]], [b_KT[m][tt]], eng="vector")

            src_h = lambda kc: hT[:, kc, :]
            run(proj_fm("w_in", 2, src_h, b_hT, ev_in))

            def rest_gen():
                yield from proj_fm("w_in", 2, src_h, b_hT, lambda sl, m, bank: ev_in(sl + 2, m, bank))
                slot = next_slab("w_in")
                for s in range(4):
                    bank = pgbank()
                    for kc in range(8):
                        P.op("tensor", lambda e, slot=slot, s=s, kc=kc, bank=bank: e.matmul(PG[bank][:], hT[:, kc, s * 128:(s + 1) * 128], ring[slot][:, kc, :], start=(kc == 0), stop=(kc == 7)),
                             reads=[b_ring[slot], b_hT[kc]], writes=[b_PG[bank]])
                    evac_copy(Vt[:, tt * 4 + s, :], PG[bank][:], [b_PG[bank]], [b_V[tt]], eng="vector")
                    yield
                release_slab()

            def lru_gen():
                T0, T1, T2, T3 = (RA[:, 4, :], RA[:, 5, :], RA[:, 6, :], RA[:, 7, :])
                bT = b_RA[4:8]
                for m in range(4):
                    G = RA[:, m, :]
                    bG = b_RA[m]
                    bank = pgbank()
                    for k in range(4):
                        P.op("tensor", lambda e, m=m, k=k, bank=bank: e.matmul(PG[bank][:], DG4[:, m, k, :], XRb[:, m, k:k + T], start=(k == 0), stop=(k == 3)),
                             reads=[b_par, b_XR[m], b_XRh], writes=[b_PG[bank]])
                    P.op("scalar", lambda e, m=m, bank=bank: e.activation(out=T0, in_=PG[bank][:], func=AF.Identity, bias=PBT[:, m, 4:5]), reads=[b_PG[bank], b_par], writes=[bT[0]])
                    P.op("vector", lambda e: e.tensor_copy(out=xcb[:], in_=T0), reads=[bT[0]], writes=[b_xcb])
                    yield
                    P.op("tensor", lambda e, m=m: e.matmul(PB[0][:], BDb[:, m, 0, :], xcb[:], start=True, stop=True), reads=[b_par, b_xcb], writes=[b_PB[0]])
                    P.op("tensor", lambda e, m=m: e.matmul(PB[1][:], BDb[:, m, 1, :], xcb[:], start=True, stop=True), reads=[b_par, b_xcb], writes=[b_PB[1]])
                    yield
                    P.op("scalar", lambda e, m=m: e.activation(out=T1, in_=PB[0][:], func=AF.Exp, scale=-1.0, bias=DER[:, m:m + 1]), reads=[b_PB[0], b_par], writes=[bT[1]])
                    P.op("scalar", lambda e: e.activation(out=T1, in_=T1, func=AF.Ln, bias=1.0), reads=[bT[1]], writes=[bT[1]])
                    P.op("scalar", lambda e: e.activation(out=T1, in_=T1, func=AF.Exp, scale=-1.0), reads=[bT[1]], writes=[bT[1]])
                    P.op("scalar", lambda e, m=m: e.activation(out=T2, in_=T1, func=AF.Exp, scale=DER[:, 8 + m:9 + m]), reads=[bT[1], b_par], writes=[bT[2]])
                    P.op("scalar", lambda e, m=m: e.activation(out=T3, in_=T1, func=AF.Exp, scale=DER[:, 12 + m:13 + m]), reads=[bT[1], b_par], writes=[bT[3]])
                    P.op("scalar", lambda e: e.activation(out=T3, in_=T3, func=AF.Ln, scale=-1.0, bias=1.0), reads=[bT[3]], writes=[bT[3]])
                    P.op("scalar", lambda e: e.activation(out=T3, in_=T3, func=AF.Exp, scale=0.5), reads=[bT[3]], writes=[bT[3]])
                    P.op("scalar", lambda e, m=m: e.activation(out=T1, in_=PB[1][:], func=AF.Exp, scale=-1.0, bias=DER[:, 4 + m:5 + m]), reads=[b_PB[1], b_par], writes=[bT[1]])
                    P.op("scalar", lambda e: e.activation(out=T1, in_=T1, func=AF.Ln, bias=1.0), reads=[bT[1]], writes=[bT[1]])
                    P.op("scalar", lambda e: e.activation(out=T1, in_=T1, func=AF.Exp, scale=-1.0), reads=[bT[1]], writes=[bT[1]])
                    P.op("vector", lambda e: e.tensor_tensor(out=T1, in0=T1, in1=T0, op=ALU.mult), reads=[bT[1], bT[0]], writes=[bT[1]])
                    P.op("vector", lambda e: e.tensor_tensor(out=T3, in0=T3, in1=T1, op=ALU.mult), reads=[bT[3], bT[1]], writes=[bT[3]])
                    P.op("vector", lambda e, m=m: e.tensor_tensor_scan(out=T0, data0=T2, data1=T3, initial=hcar[:, m:m + 1], op0=ALU.mult, op1=ALU.add),
                         reads=[bT[2], bT[3], b_hcar], writes=[bT[0]])
                    P.op("vector", lambda e, m=m: e.tensor_copy(out=hcar[:, m:m + 1], in_=T0[:, T - 1:T]), reads=[bT[0]], writes=[b_hcar])
                    P.op("vector", lambda e, m=m, yc=yc, G=G: e.tensor_tensor(out=yc[:, m, :], in0=G, in1=T0, op=ALU.mult), reads=[bG, bT[0]], writes=[b_yc[m]])
                    yield


            g_r, g_l = rest_gen(), lru_gen()
            live = [g_r, g_l]
            while live:
                for g in list(live):
                    try:
                        next(g)
                    except StopIteration:
                        live.remove(g)

        def gen_B(tt):
            yc = YC[tt % 2]
            b_yc = b_YC[tt % 2]
            pairs = []
            for j in range(4):
                blocks = [(4 * tt + i, (0 if i == 3 else 128 * i), i) for i in (3, 2, 1, 0)]
                blocks += [(kb, 0, -1) for kb in range(4 * tt - 1, -1, -1)]
                for n, (kb, c0, di) in enumerate(blocks):
                    pairs.append([dict(h=2 * j + q, q=q, kb=kb, c0=c0, di=di, first=(n == 0), last=(n == len(blocks) - 1), i=2 * len(pairs) + q) for q in range(2)])
            npairs = len(pairs)

            def qk(s_):
                h, kb, c0, i = s_["h"], s_["kb"], s_["c0"], s_["i"]
                j, r0 = h // 2, (h % 2) * 64
                bk = i % NPB
                P.op("tensor", lambda e: e.matmul(PB[bk][:, c0:T], KT[r0:r0 + 64, j, kb * 128:(kb + 1) * 128], QT[r0:r0 + 64, j, c0:T], start=True, stop=False, skip_group_check=True),
                     reads=[b_KT[j][kb // 4], b_QT[j]], writes=[b_PB[bk]])

            def mask(s_):
                c0, i = s_["c0"], s_["i"]
                bk = i % NPB
                if s_["di"] == 3:
                    P.op("tensor", lambda e: e.matmul(PB[bk][:, 0:T], IDB, NEGF3, start=False, stop=False, skip_group_check=True), reads=[b_cst], writes=[b_PB[bk]])
                elif s_["di"] >= 0:
                    P.op("tensor", lambda e: e.matmul(PB[bk][:, c0:c0 + 128], IDB, NEGTRI, start=False, stop=False, skip_group_check=True), reads=[b_cst], writes=[b_PB[bk]])

            def e_op(pa_):
                c0, i = pa_[0]["c0"], pa_[0]["i"]
                pp = (i // 2) % 2
                P.op("scalar", lambda e: e.activation(out=e2[:, :, c0:T], in_=PBt[:, 2 * pp:2 * pp + 2, c0:T], func=AF.Exp), reads=[b_PB[2 * pp], b_PB[2 * pp + 1]], writes=[b_e[0]])

            def l_op(pa_):
                c0, i = pa_[0]["c0"], pa_[0]["i"]
                pp = (i // 2) % 2
                P.op("scalar", lambda e: e.activation(out=sp2[pp][:, :, c0:T], in_=e2[:, :, c0:T], func=AF.Ln, bias=1.0), reads=[b_e[0]], writes=[b_sp[pp]])

            def tri(s_):
                c0, i, q = s_["c0"], s_["i"], s_["q"]
                bk, pp = i % NPB, (i // 2) % 2
                P.op("tensor", lambda e: e.matmul(PB[bk][:, c0:T], NTRI, sp2[pp][:, q, c0:T], start=False, stop=s_["first"], skip_group_check=True),
                     reads=[b_sp[pp], b_cst], writes=[b_PB[bk]])
                if not s_["first"]:
                    P.op("tensor", lambda e: e.matmul(PB[bk][:, c0:T], nonesb[:], Srun[q][:, c0:T], start=False, stop=True, skip_group_check=True),
                         reads=[b_S[q], b_cst], writes=[b_PB[bk]])

            def supd(s_):
                c0, i, q = s_["c0"], s_["i"], s_["q"]
                pp = (i // 2) % 2
                if not s_["last"]:
                    if s_["first"]:
                        P.op("vector", lambda e: e.tensor_copy(out=Srun[q][:], in_=sp2[pp][:, q, :]), reads=[b_sp[pp]], writes=[b_S[q]])
                    else:
                        P.op("vector", lambda e: e.tensor_tensor(out=Srun[q][:, c0:T], in0=Srun[q][:, c0:T], in1=sp2[pp][:, q, c0:T], op=ALU.add), reads=[b_sp[pp], b_S[q]], writes=[b_S[q]])

            def xw(pa_):
                c0, i = pa_[0]["c0"], pa_[0]["i"]
                pp = (i // 2) % 2
                P.op("scalar", lambda e: e.activation(out=w2[pp][:, :, c0:T], in_=PBt[:, 2 * pp:2 * pp + 2, c0:T], func=AF.Exp), reads=[b_PB[2 * pp], b_PB[2 * pp + 1]], writes=[b_w[pp]])

            def wv(s_):
                h, kb, c0, i, q = s_["h"], s_["kb"], s_["c0"], s_["i"], s_["q"]
                j, r0 = h // 2, (h % 2) * 64
                pp = (i // 2) % 2
                po = j % NPO
                P.op("tensor", lambda e: e.matmul(PO[po][r0:r0 + 64, c0:T], Vt[:, kb, h * 64:(h + 1) * 64], w2[pp][:, q, c0:T], start=s_["first"], stop=s_["last"], skip_group_check=True),
                     reads=[b_V[kb // 4], b_w[pp]], writes=[b_PO[po]])
                if s_["last"] and h % 2 == 1:
                    evac_copy(yc[:, 4 + j, :], PO[po][:], [b_PO[po]], [b_yc[4 + j]], eng="vector")

            for it in range(npairs + 2):
                P.tile = tt
                if it < npairs:
                    pa_ = pairs[it]
                    qk(pa_[0]); qk(pa_[1])
                    mask(pa_[0]); mask(pa_[1])
                    e_op(pa_)
                if 0 <= it - 1 < npairs:
                    pa_ = pairs[it - 1]
                    tri(pa_[0]); tri(pa_[1])
                    supd(pa_[0]); supd(pa_[1])
                    xw(pa_)
                if it < npairs:
                    l_op(pairs[it])
                if 0 <= it - 2 < npairs:
                    pa_ = pairs[it - 2]
                    wv(pa_[0]); wv(pa_[1])
                yield

        def gen_C(tt):
            t0 = tt * T
            yc = YC[tt % 2]
            b_yc = b_YC[tt % 2]

            def st():
                P.tile = tt

            st()
            yield from load_x(tt)

            def ev_res(sl, m, bank):
                st()
                c = sl * 4 + m
                P.op("vector", lambda e: e.tensor_tensor(out=xT[:, c, :], in0=xT[:, c, :], in1=PG[bank][:], op=ALU.add), reads=[b_PG[bank], b_xT[c]], writes=[b_xT[c]])

            st()
            yield from proj_fm("w_out", 2, lambda kc: yc[:, kc, :], b_yc, ev_res)

            def mlp(grow):
                st()
                rms_norm()
                norm_apply_h(grow)
                yield
                for half in range(2):
                    def ev_w1(sl, m, bank):
                        st()
                        k = sl * 4 + m
                        q = k % 2
                        if k % 2 == 0:
                            P.op("scalar", lambda e: e.activation(out=ce_b[q][:], in_=PG[bank][:], func=AF.Square), reads=[b_PG[bank]], writes=[b_ce[q]])
                            P.op("vector", lambda e: e.scalar_tensor_tensor(out=aT_chunk(k), in0=PG[bank][:], scalar=0.0, in1=ce_b[q][:], op0=ALU.is_gt, op1=ALU.mult),
                                 reads=[b_PG[bank], b_ce[q]], writes=[b_RA[k // 2]])
                        else:
                            P.op("vector", lambda e: e.tensor_scalar(out=ce_b[q][:], in0=PG[bank][:], scalar1=0.0, scalar2=None, op0=ALU.max), reads=[b_PG[bank]], writes=[b_ce[q]])
                            P.op("vector", lambda e: e.tensor_tensor(out=aT_chunk(k), in0=ce_b[q][:], in1=ce_b[q][:], op=ALU.mult), reads=[b_ce[q]], writes=[b_RA[k // 2]])
                    st()
                    yield from proj_fm("w1", 4, lambda kc: hT[:, kc, :], b_hT, ev_w1)
                    for q4 in range(4):
                        slot = next_slab("w2")
                        wv_ = ring[slot][:].rearrange("p a b -> p (a b)").rearrange("p (kc c) -> p kc c", c=256)
                        banks = [pgbank(), pgbank()]
                        for mm in range(2):
                            for kq in range(2):
                                st()
                                for kc in range(kq * 8, kq * 8 + 8):
                                    P.op("tensor", lambda e, wv_=wv_, mm=mm, kc=kc, bank=banks[mm]: e.matmul(PG[bank][:], wv_[:, kc, mm * 128:(mm + 1) * 128], aT_chunk(kc), start=(kc == 0), stop=(kc == 15)),
                                         reads=[b_ring[slot], b_RA[kc // 2]], writes=[b_PG[banks[mm]]])
                                yield
                        for mm in range(2):
                            st()
                            c = q4 * 2 + mm
                            P.op("vector", lambda e, c=c, bank=banks[mm]: e.tensor_tensor(out=xT[:, c, :], in0=xT[:, c, :], in1=PG[bank][:], op=ALU.add), reads=[b_PG[banks[mm]], b_xT[c]], writes=[b_xT[c]])
                        release_slab()

            yield from mlp(R_MG0)

            st()
            rms_norm()
            norm_apply_h(R_ODG)
            P.op("vector", lambda e: e.tensor_copy(out=YU[:, :, 0:30], in_=UH[:]), reads=[b_UH], writes=b_YU)
            yield

            def ev_pw1(sl, m, bank):
                st()
                c = (sl % 2) * 4 + m
                if sl < 2:
                    P.op("vector", lambda e: e.tensor_scalar(out=aT_chunk(c), in0=PG[bank][:], scalar1=PAT[:, c, R_PW1B0:R_PW1B0 + 1], scalar2=None, op0=ALU.add),
                         reads=[b_PG[bank], b_par], writes=[b_RA[c // 2]])
                else:
                    q = c % 2
                    P.op("scalar", lambda e: e.activation(out=ce_b[q][:], in_=PG[bank][:], func=AF.Exp, scale=-1.0, bias=DER[:, 16 + c:17 + c]), reads=[b_PG[bank], b_par], writes=[b_ce[q]])
                    P.op("scalar", lambda e: e.activation(out=ce_b[q][:], in_=ce_b[q][:], func=AF.Ln, bias=1.0), reads=[b_ce[q]], writes=[b_ce[q]])
                    P.op("scalar", lambda e: e.activation(out=ce_b[q][:], in_=ce_b[q][:], func=AF.Exp, scale=-1.0), reads=[b_ce[q]], writes=[b_ce[q]])
                    P.op("vector", lambda e: e.tensor_tensor(out=YU[:, c, 30:T + 30], in0=aT_chunk(c), in1=ce_b[q][:], op=ALU.mult), reads=[b_ce[q], b_RA[c // 2]], writes=[b_YU[c]])

            yield from proj_fm("pw1", 4, lambda kc: hT[:, kc, :], b_hT, ev_pw1)
            st()
            P.op("vector", lambda e: e.tensor_copy(out=UH[:], in_=YU[:, :, T:T + 30]), reads=b_YU, writes=[b_UH])

            dgi = {"n": 0}
            for c in range(8):
                bank = pgbank()
                for k in range(31):
                    st()
                    di = dgi["n"] % NDG
                    dgi["n"] += 1
                    P.op("vector", lambda e, c=c, k=k, di=di: e.tensor_scalar(out=dg31[:, di, :], in0=IDB, scalar1=PAT[:, c, R_DWW + k:R_DWW + k + 1], scalar2=None, op0=ALU.mult),
                         reads=[b_par, b_cst], writes=[b_dg[di]])
                    P.op("tensor", lambda e, c=c, k=k, di=di, bank=bank: e.matmul(PG[bank][:], dg31[:, di, :], YU[:, c, k:k + T], start=(k == 0), stop=(k == 30)),
                         reads=[b_dg[di], b_YU[c]], writes=[b_PG[bank]])
                    if k % 8 == 7:
                        yield
                st()
                P.op("scalar", lambda e, c=c, bank=bank: e.activation(out=RA[:, c, :], in_=PG[bank][:], func=AF.Identity, bias=PAT[:, c, R_DWB:R_DWB + 1]), reads=[b_PG[bank], b_par], writes=[b_RA[c]])
                yield

            st()
            bsum = pgbank()
            for c in range(8):
                q = c % 2
                P.op("vector", lambda e, c=c, q=q: e.tensor_copy(out=sq_b[q][:], in_=RA[:, c, :]), reads=[b_RA[c]], writes=[b_sq[q]])
                P.op("tensor", lambda e, c=c, q=q, bsum=bsum: e.matmul(PG[bsum][:], onesb[:], sq_b[q][:], start=(c == 0), stop=(c == 7)), reads=[b_sq[q], b_cst], writes=[b_PG[bsum]])
            yield
            st()
            bsq = pgbank()
            for c in range(8):
                q = c % 2
                P.op("scalar", lambda e, c=c, q=q: e.activation(out=cw_b[q][:], in_=RA[:, c, :], func=AF.Square), reads=[b_RA[c]], writes=[b_cw[q]])
                P.op("tensor", lambda e, c=c, q=q, bsq=bsq: e.matmul(PG[bsq][:], onesb[:], cw_b[q][:], start=(c == 0), stop=(c == 7)), reads=[b_cw[q], b_cst], writes=[b_PG[bsq]])
            yield
            st()
            P.op("vector", lambda e, bsum=bsum: e.tensor_scalar(out=lnv[:], in0=PG[bsum][:], scalar1=1.0 / D, scalar2=None, op0=ALU.mult), reads=[b_PG[bsum]], writes=[b_lnv])
            P.op("vector", lambda e: e.tensor_tensor(out=ce_b[0][:], in0=lnv[:], in1=lnv[:], op=ALU.mult), reads=[b_lnv], writes=[b_ce[0]])
            P.op("vector", lambda e, bsq=bsq: e.scalar_tensor_tensor(out=ce_b[0][:], in0=PG[bsq][:], scalar=1.0 / D, in1=ce_b[0][:], op0=ALU.mult, op1=ALU.subtract), reads=[b_PG[bsq], b_ce[0]], writes=[b_ce[0]])
            P.op("scalar", lambda e: e.activation(out=ce_b[0][:], in_=ce_b[0][:], func=AF.Ln, bias=1e-5), reads=[b_ce[0]], writes=[b_ce[0]])
            P.op("scalar", lambda e: e.activation(out=rstd[:], in_=ce_b[0][:], func=AF.Exp, scale=-0.5), reads=[b_ce[0]], writes=[b_rstd])
            for c in range(8):
                st()
                q = c % 2
                P.op("vector", lambda e, c=c: e.tensor_tensor(out=RA[:, c, :], in0=RA[:, c, :], in1=lnv[:], op=ALU.subtract), reads=[b_RA[c], b_lnv], writes=[b_RA[c]])
                P.op("vector", lambda e, c=c: e.tensor_tensor(out=RA[:, c, :], in0=RA[:, c, :], in1=rstd[:], op=ALU.mult), reads=[b_RA[c], b_rstd], writes=[b_RA[c]])
                P.op("scalar", lambda e, c=c, q=q: e.activation(out=ce_b[q][:], in_=RA[:, c, :], func=AF.Exp, scale=DER[:, 24 + c:25 + c], bias=DER[:, 32 + c:33 + c]), reads=[b_RA[c], b_par], writes=[b_ce[q]])
                P.op("scalar", lambda e, q=q: e.activation(out=ce_b[q][:], in_=ce_b[q][:], func=AF.Ln, bias=1.0), reads=[b_ce[q]], writes=[b_ce[q]])
                P.op("scalar", lambda e, q=q: e.activation(out=ce_b[q][:], in_=ce_b[q][:], func=AF.Exp, scale=-1.0), reads=[b_ce[q]], writes=[b_ce[q]])
                P.op("vector", lambda e, c=c: e.tensor_scalar(out=RA[:, c, :], in0=RA[:, c, :], scalar1=PAT[:, c, R_LNG:R_LNG + 1], scalar2=PAT[:, c, R_LNB:R_LNB + 1], op0=ALU.mult, op1=ALU.add),
                     reads=[b_RA[c], b_par], writes=[b_RA[c]])
                P.op("vector", lambda e, c=c, q=q: e.tensor_tensor(out=hT[:, c, :], in0=RA[:, c, :], in1=ce_b[q][:], op=ALU.mult), reads=[b_RA[c], b_ce[q]], writes=[b_hT[c]])
                yield

            def ev_pw2(sl, m, bank):
                st()
                c = sl * 4 + m
                P.op("vector", lambda e: e.scalar_tensor_tensor(out=xT[:, c, :], in0=PG[bank][:], scalar=PAT[:, c, R_PW2B:R_PW2B + 1], in1=xT[:, c, :], op0=ALU.add, op1=ALU.add),
                     reads=[b_PG[bank], b_xT[c], b_par], writes=[b_xT[c]])

            yield from proj_fm("pw2", 2, lambda kc: hT[:, kc, :], b_hT, ev_pw2)

            yield from mlp(R_MG1)

            st()
            rms_norm()
            for c in range(8):
                P.op("vector", lambda e, c=c: e.scalar_tensor_tensor(out=RA[:, c, :], in0=xT[:, c, :], scalar=PAT[:, c, R_FG:R_FG + 1], in1=rstd[:], op0=ALU.mult, op1=ALU.mult),
                     reads=[b_xT[c], b_rstd, b_par], writes=[b_RA[c]])
            yield
            for s in range(4):
                st()
                oi = s % 2
                for half in range(2):
                    bank = pgbank()
                    for cc in range(4):
                        c = half * 4 + cc
                        P.op("tensor", lambda e, c=c, cc=cc, s=s, bank=bank: e.transpose(out=PG[bank][:, cc * 128:(cc + 1) * 128], in_=RA[:, c, s * 128:(s + 1) * 128], identity=IDF),
                             reads=[b_RA[c], b_cst], writes=[b_PG[bank]])
                    evac_copy(ost[oi][:, half * 512:(half + 1) * 512], PG[bank][:], [b_PG[bank]], [b_ost[oi]])
                dma("gpsimd", y_d[t0 + s * 128:t0 + (s + 1) * 128, :], ost[oi][:], ost_sem[oi], reads=[b_ost[oi]])
                yield

        NC_EST = 214.0

        def interleave(gb, gc, nb):
            import os
            if os.environ.get("KSEQ") == "1":
                run(gb)
                run(gc)
                return
            if os.environ.get("KSEQ") == "2":
                run(gc)
                run(gb)
                return
            ratio = NC_EST / max(nb, 1)
            acc = 0.0
            c_done = False
            for _ in range(nb):
                next(gb, None)
                acc += ratio
                while acc >= 1.0 and not c_done:
                    acc -= 1.0
                    try:
                        next(gc)
                    except StopIteration:
                        c_done = True
            run(gb)
            if not c_done:
                run(gc)

        stage_A(0)
        run(gen_B(0))
        for tt in range(NT):
            if tt + 1 < NT:
                stage_A(tt + 1)
                nb = 4 * (4 * (tt + 1) + 4) + 2
                interleave(gen_B(tt + 1), gen_C(tt), nb)
            else:
                run(gen_C(tt))

        P.tile = NT
        last = [o for o in P.ops["gpsimd"] if o.dma and o.sem in ost_sem]
        P.op("gpsimd", lambda e: e.nop(), extra=last[-2:])

        P.finalize(engsems)
        blk = es.enter_context(nc.Block())

        @blk.sync
        def _(e):
            P.emit("sync", e)

        @blk.scalar
        def _(e):
            P.emit("scalar", e)

        @blk.gpsimd
        def _(e):
            P.emit("gpsimd", e)

        @blk.vector
        def _(e):
            P.emit("vector", e)

        @blk.tensor
        def _(e):
            P.emit("tensor", e)

    return nc


def make_consts():
    c = np.zeros((128, 896), np.float32)
    c[:, 0:128] = np.eye(128, dtype=np.float32)
    j = np.arange(128)[:, None]
    s = np.arange(128)[None, :]
    c[:, 128:256] = np.where(j >= s, -1.0, 0.0)
    c[:, 256:384] = np.where(j >= s, NEG, 0.0)
    c[:, 384:768] = NEG
    c[:, 768:896] = c[:, 256:384]
    return c


def pack_inputs(inp, NT=8):
    f = lambda a: np.ascontiguousarray(np.asarray(a, dtype=np.float32))
    pa = np.zeros((NPA, D), np.float32)
    pa[R_EVG] = inp["ev_norm_g"][0]
    pa[R_ODG] = inp["od_norm_g"][0]
    pa[R_MG0] = inp["mlp_norm_g"][0]
    pa[R_MG1] = inp["mlp_norm_g"][1]
    pa[R_FG] = inp["final_g"]
    pa[R_DWB] = inp["od_dw_b"][0]
    pa[R_LNG] = inp["od_ln_g"][0]
    pa[R_LNB] = inp["od_ln_b"][0]
    pa[R_PW2B] = inp["od_pw2_b"][0]
    pa[R_PW1B0] = inp["od_pw1_b"][0][:D]
    pa[R_PW1B1] = inp["od_pw1_b"][0][D:]
    pa[R_DWW:R_DWW + 31] = inp["od_dw_w"][0]
    pb = np.zeros((8, 512), np.float32)
    pb[0:4] = inp["ev_conv_w"][0]
    pb[4] = inp["ev_conv_b"][0]
    pb[5] = inp["ev_gate_a_b"][0]
    pb[6] = inp["ev_gate_x_b"][0]
    pb[7] = inp["ev_lam"][0]
    shared = {
        "w_in": f(inp["ev_w_in"][0]), "w_out": f(inp["ev_w_out"][0]),
        "w1": f(inp["mlp_w1"]), "w2": f(inp["mlp_w2"]),
        "pw1": f(inp["od_pw1_w"][0]), "pw2": f(inp["od_pw2_w"][0]),
        "pa": pa, "pb": pb, "gaw": f(inp["ev_gate_a_w"][0]), "gxw": f(inp["ev_gate_x_w"][0]),
        "cst": make_consts(),
    }
    x = f(inp["x"])
    return [dict(shared, x=x[b]) for b in range(x.shape[0])]


_NC_CACHE = {}


def kernel(**inputs):
    inputs = {k: np.asarray(v) for k, v in inputs.items()}
    if "nc" not in _NC_CACHE:
        _NC_CACHE["nc"] = build(8)
    nc = _NC_CACHE["nc"]
    in_maps = pack_inputs(inputs)
    res = run_bass_kernel_spmd(nc, in_maps, core_ids=list(range(8)))
    return np.stack([np.asarray(r["y"], dtype=np.float32) for r in res.results], axis=0)
```
